# Optimizing a Trainium2 kernel written in Bass

```python
import math
import jax, jax.numpy as jnp
from jax import lax
import numpy as np

D_MODEL = 1024
BATCH = 16
SEQ = 2048
DEPTH = 4

GRID_W = 64
CTX_LEN = 256
N_MIXERS = 3
N_LAYERS_A = (DEPTH + 2) // 3
N_LAYERS_B = (DEPTH + 1) // 3
N_LAYERS_C = DEPTH // 3
HEAD_DIM = 64
DIFF_HEADS = D_MODEL // (2 * HEAD_DIM)
DIFF_V_DIM = 2 * HEAD_DIM
GQA_HEADS = D_MODEL // HEAD_DIM
GQA_KV_HEADS = GQA_HEADS // 4
GQA_GROUP = GQA_HEADS // GQA_KV_HEADS
Q_BLOCK = 128
WINDOW = 128
D_FF = 256 * ((8 * D_MODEL // 3 + 255) // 256)
CONV_W = 3
ROPE_THETA = 10000.0
EPS = 1e-6
ATTN_SCALE = HEAD_DIM ** -0.5

kernel_name = 'hybrid_diff_grid_window_convffn_trunk'


def rms_norm(x, g):
    xf = x.astype(jnp.float32)
    y = xf * lax.rsqrt(jnp.mean(xf * xf, axis=-1, keepdims=True) + EPS)
    return (y * g.astype(jnp.float32)).astype(x.dtype)


def modulate(h, shift, scale):
    return h * (1.0 + scale) + shift


def grid_rope_tables(rows):
    t = jnp.arange(rows * GRID_W)
    row = (t // GRID_W).astype(jnp.float32)
    col = (t % GRID_W).astype(jnp.float32)
    n_freq = HEAD_DIM // 4
    inv_freq = ROPE_THETA ** (-jnp.arange(n_freq, dtype=jnp.float32) / n_freq)
    ang = jnp.concatenate([row[:, None] * inv_freq, col[:, None] * inv_freq], axis=-1)
    return jnp.cos(ang), jnp.sin(ang)


def apply_rope(x, cos, sin):
    half = x.shape[-1] // 2
    shape = (1, cos.shape[0]) + (1,) * (x.ndim - 3) + (cos.shape[1],)
    cs = cos.reshape(shape).astype(x.dtype)
    sn = sin.reshape(shape).astype(x.dtype)
    x1, x2 = x[..., :half], x[..., half:]
    return jnp.concatenate([x1 * cs - x2 * sn, x2 * cs + x1 * sn], axis=-1)


def to_blocks(t):
    b, l = t.shape[:2]
    t = t.reshape((b, l // Q_BLOCK, Q_BLOCK) + t.shape[2:])
    return jnp.moveaxis(t, 1, 0)


def from_blocks(t):
    t = jnp.moveaxis(t, 0, 1)
    return t.reshape((t.shape[0], t.shape[1] * t.shape[2]) + t.shape[3:])


def diff_proj(h, w_qkv, qk_g):
    b, l, _ = h.shape
    q, k, v = jnp.split(h @ w_qkv, 3, axis=-1)
    q = rms_norm(q.reshape(b, l, DIFF_HEADS, 2, HEAD_DIM), qk_g[0])
    k = rms_norm(k.reshape(b, l, DIFF_HEADS, 2, HEAD_DIM), qk_g[1])
    v = v.reshape(b, l, DIFF_HEADS, DIFF_V_DIM)
    return q, k, v


def diff_core(q, k, v, lam):
    s = jnp.einsum('bqhcd,bshcd->bhcqs', q, k).astype(jnp.float32) * ATTN_SCALE
    p = jax.nn.softmax(s, axis=-1)
    w = (p[:, :, 0] - lam * p[:, :, 1]).astype(v.dtype)
    return jnp.einsum('bhqs,bshe->bqhe', w, v)


def diff_attention(h_lat, h_ctx, w_qkv, qk_g, lam_p, head_g, w_o, lam_init, cos, sin, need_ctx):
    q_l, k_l, v_l = diff_proj(h_lat, w_qkv, qk_g)
    q_c, k_c, v_c = diff_proj(h_ctx, w_qkv, qk_g)
    q_l = apply_rope(q_l, cos, sin)
    k_l = apply_rope(k_l, cos, sin)
    lq = lam_p.astype(jnp.float32)
    lam = jnp.exp(jnp.sum(lq[0] * lq[1])) - jnp.exp(jnp.sum(lq[2] * lq[3])) + lam_init
    k_all = jnp.concatenate([k_l, k_c], axis=1)
    v_all = jnp.concatenate([v_l, v_c], axis=1)
    o_l = from_blocks(lax.map(lambda qb: diff_core(qb, k_all, v_all, lam), to_blocks(q_l)))

    def finish(o):
        b, l = o.shape[:2]
        o = rms_norm(o, head_g) * (1.0 - lam_init)
        return o.reshape(b, l, DIFF_HEADS * DIFF_V_DIM) @ w_o

    y_l = finish(o_l)
    y_c = finish(diff_core(q_c, k_c, v_c, lam)) if need_ctx else None
    return y_l, y_c


def gqa_proj(h, w_qkv, qk_g):
    b, l, _ = h.shape
    q, k, v = jnp.split(h @ w_qkv, [GQA_HEADS * HEAD_DIM, (GQA_HEADS + GQA_KV_HEADS) * HEAD_DIM], axis=-1)
    q = rms_norm(q.reshape(b, l, GQA_KV_HEADS, GQA_GROUP, HEAD_DIM), qk_g[0])
    k = rms_norm(k.reshape(b, l, GQA_KV_HEADS, HEAD_DIM), qk_g[1])
    v = v.reshape(b, l, GQA_KV_HEADS, HEAD_DIM)
    return q, k, v


def merge_heads(o):
    return o.reshape(o.shape[0], o.shape[1], GQA_HEADS * HEAD_DIM)


def gqa_dense_core(q, k, v):
    s = jnp.einsum('bqkgd,bskd->bkgqs', q, k).astype(jnp.float32) * ATTN_SCALE
    p = jax.nn.softmax(s, axis=-1).astype(v.dtype)
    return jnp.einsum('bkgqs,bskd->bqkgd', p, v)


def grid_gqa_attention(h_lat, h_ctx, w_qkv, qk_g, w_o, cos, sin, need_ctx):
    q_l, k_l, v_l = gqa_proj(h_lat, w_qkv, qk_g)
    q_c, k_c, v_c = gqa_proj(h_ctx, w_qkv, qk_g)
    q_l = apply_rope(q_l, cos, sin)
    k_l = apply_rope(k_l, cos, sin)
    k_all = jnp.concatenate([k_l, k_c], axis=1)
    v_all = jnp.concatenate([v_l, v_c], axis=1)
    o_l = from_blocks(lax.map(lambda qb: gqa_dense_core(qb, k_all, v_all), to_blocks(q_l)))
    y_l = merge_heads(o_l) @ w_o
    y_c = merge_heads(gqa_dense_core(q_c, k_c, v_c)) @ w_o if need_ctx else None
    return y_l, y_c


def window_gqa_attention(h_lat, h_ctx, w_qkv, qk_g, sink, w_o, cos, sin, need_ctx):
    q_l, k_l, v_l = gqa_proj(h_lat, w_qkv, qk_g)
    q_c, k_c, v_c = gqa_proj(h_ctx, w_qkv, qk_g)
    q_l = apply_rope(q_l, cos, sin)
    k_l = apply_rope(k_l, cos, sin)
    n_lat = h_lat.shape[1]
    n_ctx = h_ctx.shape[1]
    band = Q_BLOCK + 2 * WINDOW
    pad = ((0, 0), (WINDOW, WINDOW), (0, 0), (0, 0))
    k_pad = jnp.pad(k_l, pad)
    v_pad = jnp.pad(v_l, pad)
    sink_f = sink.astype(jnp.float32).reshape(GQA_KV_HEADS, GQA_GROUP)[None, :, :, None, None]
    q_off = jnp.arange(Q_BLOCK)[:, None]
    k_off = jnp.arange(band)[None, :]
    rel = k_off - q_off
    in_band = (rel >= 0) & (rel <= 2 * WINDOW)

    def block(args):
        i, qb = args
        kb = lax.dynamic_slice_in_dim(k_pad, i * Q_BLOCK, band, axis=1)
        vb = lax.dynamic_slice_in_dim(v_pad, i * Q_BLOCK, band, axis=1)
        kpos = i * Q_BLOCK - WINDOW + k_off
        mask = in_band & (kpos >= 0) & (kpos < n_lat)
        s_w = jnp.einsum('bqkgd,bskd->bkgqs', qb, kb).astype(jnp.float32) * ATTN_SCALE
        s_w = jnp.where(mask, s_w, -jnp.inf)
        s_c = jnp.einsum('bqkgd,bskd->bkgqs', qb, k_c).astype(jnp.float32) * ATTN_SCALE
        s_s = jnp.broadcast_to(sink_f, s_w.shape[:-1] + (1,))
        p = jax.nn.softmax(jnp.concatenate([s_w, s_c, s_s], axis=-1), axis=-1).astype(vb.dtype)
        return (jnp.einsum('bkgqs,bskd->bqkgd', p[..., :band], vb)
                + jnp.einsum('bkgqs,bskd->bqkgd', p[..., band:band + n_ctx], v_c))

    nb = n_lat // Q_BLOCK
    o_l = from_blocks(lax.map(block, (jnp.arange(nb), to_blocks(q_l))))
    y_l = merge_heads(o_l) @ w_o
    y_c = None
    if need_ctx:
        s = jnp.einsum('bqkgd,bskd->bkgqs', q_c, k_c).astype(jnp.float32) * ATTN_SCALE
        s_s = jnp.broadcast_to(sink_f, s.shape[:-1] + (1,))
        p = jax.nn.softmax(jnp.concatenate([s, s_s], axis=-1), axis=-1)[..., :n_ctx].astype(v_c.dtype)
        y_c = merge_heads(jnp.einsum('bkgqs,bskd->bqkgd', p, v_c)) @ w_o
    return y_l, y_c


def conv_ffn(h, w_up, conv_w, conv_b, w_down):
    n = h.shape[1]
    u = h @ w_up
    half = CONV_W // 2
    up = jnp.pad(u, ((0, 0), (half, half), (0, 0)))
    u = sum(up[:, j:j + n] * conv_w[j] for j in range(CONV_W)) + conv_b
    gate, val = jnp.split(u, 2, axis=-1)
    return (jax.nn.silu(gate) * val) @ w_down


def setup_inputs(seed: int = 0) -> dict:
    key = jax.random.key(seed)
    ks = iter(jax.random.split(key, 32))
    f32 = jnp.float32
    nrm = lambda shape, s: jax.random.normal(next(ks), shape, f32) * s
    gain = lambda shape: 1.0 + 0.02 * jax.random.normal(next(ks), shape, f32)
    d = D_MODEL
    qkv_gqa = (GQA_HEADS + 2 * GQA_KV_HEADS) * HEAD_DIM
    return {
        'x': nrm((BATCH, SEQ, d), 1.0),
        'c': nrm((BATCH, d), 1.0),
        'ctx': nrm((BATCH, CTX_LEN, d), 1.0),
        'c_ctx': nrm((d,), 1.0),
        'adaln_w': nrm((DEPTH, d, 6 * d), 0.5 * d ** -0.5),
        'adaln_b': nrm((DEPTH, 6 * d), 0.01),
        'norm1_g': gain((DEPTH, d)),
        'norm2_g': gain((DEPTH, d)),
        'ffn_w_up': nrm((DEPTH, d, 2 * D_FF), d ** -0.5),
        'ffn_conv_w': nrm((DEPTH, CONV_W, 2 * D_FF), CONV_W ** -0.5),
        'ffn_conv_b': nrm((DEPTH, 2 * D_FF), 0.01),
        'ffn_w_down': nrm((DEPTH, D_FF, d), D_FF ** -0.5),
        'a_w_qkv': nrm((N_LAYERS_A, d, 3 * DIFF_HEADS * 2 * HEAD_DIM), d ** -0.5),
        'a_qk_g': gain((N_LAYERS_A, 2, HEAD_DIM)),
        'a_lambda': nrm((N_LAYERS_A, 4, HEAD_DIM), 0.1),
        'a_head_g': gain((N_LAYERS_A, DIFF_V_DIM)),
        'a_w_o': nrm((N_LAYERS_A, DIFF_HEADS * DIFF_V_DIM, d), (DIFF_HEADS * DIFF_V_DIM) ** -0.5),
        'b_w_qkv': nrm((N_LAYERS_B, d, qkv_gqa), d ** -0.5),
        'b_qk_g': gain((N_LAYERS_B, 2, HEAD_DIM)),
        'b_w_o': nrm((N_LAYERS_B, GQA_HEADS * HEAD_DIM, d), (GQA_HEADS * HEAD_DIM) ** -0.5),
        'c_w_qkv': nrm((N_LAYERS_C, d, qkv_gqa), d ** -0.5),
        'c_qk_g': gain((N_LAYERS_C, 2, HEAD_DIM)),
        'c_sink': nrm((N_LAYERS_C, GQA_HEADS), 0.5),
        'c_w_o': nrm((N_LAYERS_C, GQA_HEADS * HEAD_DIM, d), (GQA_HEADS * HEAD_DIM) ** -0.5),
    }


def reference(x, c, ctx, c_ctx, adaln_w, adaln_b, norm1_g, norm2_g, ffn_w_up, ffn_conv_w, ffn_conv_b,
              ffn_w_down, a_w_qkv, a_qk_g, a_lambda, a_head_g, a_w_o, b_w_qkv, b_qk_g, b_w_o,
              c_w_qkv, c_qk_g, c_sink, c_w_o):
    ROWS = x.shape[1] // GRID_W
    cos, sin = grid_rope_tables(ROWS)
    h_ctx = ctx
    sc = jax.nn.silu(c)
    sc_ctx = jax.nn.silu(c_ctx)
    for i in range(DEPTH):
        last = i == DEPTH - 1
        j = i // N_MIXERS
        kind = i % N_MIXERS
        mod_l = (sc @ adaln_w[i] + adaln_b[i])[:, None, :]
        mod_c = (sc_ctx @ adaln_w[i] + adaln_b[i])[None, None, :]
        sh1, sc1, g1, sh2, sc2, g2 = jnp.split(mod_l, 6, axis=-1)
        csh1, csc1, cg1, csh2, csc2, cg2 = jnp.split(mod_c, 6, axis=-1)
        hn_l = modulate(rms_norm(x, norm1_g[i]), sh1, sc1)
        hn_c = modulate(rms_norm(h_ctx, norm1_g[i]), csh1, csc1)
        if kind == 0:
            lam_init = 0.8 - 0.6 * math.exp(-0.3 * i)
            y_l, y_c = diff_attention(hn_l, hn_c, a_w_qkv[j], a_qk_g[j], a_lambda[j], a_head_g[j], a_w_o[j],
                                      lam_init, cos, sin, not last)
        elif kind == 1:
            y_l, y_c = grid_gqa_attention(hn_l, hn_c, b_w_qkv[j], b_qk_g[j], b_w_o[j], cos, sin, not last)
        else:
            y_l, y_c = window_gqa_attention(hn_l, hn_c, c_w_qkv[j], c_qk_g[j], c_sink[j], c_w_o[j],
                                            cos, sin, not last)
        x = x + g1 * y_l
        x = x + g2 * conv_ffn(modulate(rms_norm(x, norm2_g[i]), sh2, sc2),
                              ffn_w_up[i], ffn_conv_w[i], ffn_conv_b[i], ffn_w_down[i])
        if not last:
            h_ctx = h_ctx + cg1 * y_c
            h_ctx = h_ctx + cg2 * conv_ffn(modulate(rms_norm(h_ctx, norm2_g[i]), csh2, csc2),
                                           ffn_w_up[i], ffn_conv_w[i], ffn_conv_b[i], ffn_w_down[i])
    return x
```

```python
import math
import numpy as np
from contextlib import ExitStack
import concourse.bass as bass
import concourse.mybir as mybir
from concourse.bass_utils import run_bass_kernel_spmd

F32 = mybir.dt.float32
BF16 = mybir.dt.bfloat16
ALU = mybir.AluOpType
AF = mybir.ActivationFunctionType
AX = mybir.AxisListType

NB = 2
T = 2048
C = 256
NT = T + C
D = 1024
KC = 8
DFF = 2816
FC = 22
DEPTH = 4
EPS = 1e-6
TCH = [(0, 512), (512, 512), (1024, 512), (1536, 512), (2048, 256)]
NTILE = NT // 128


class Trk:
    __slots__ = ("w", "r")

    def __init__(self):
        self.w = None
        self.r = []


class Prog:
    def __init__(self, nc, es):
        self.nc = nc
        self.es = es
        self.engs = {"pe": nc.tensor, "act": nc.scalar, "dve": nc.vector,
                     "pool": nc.gpsimd, "sp": nc.sync}
        self.sems = {}
        self.cnt = {}
        self.known = {k: {} for k in self.engs}
        for k in self.engs:
            if k != "sp":
                self._mksem("e_" + k)
        self.n_wait = 0
        self.n_ins = 0

    def _mksem(self, key):
        h = self.es.enter_context(self.nc.semaphore(key))
        self.sems[key] = h
        self.cnt[key] = 0
        return h

    def sbuf(self, name, shape, dt):
        return self.es.enter_context(self.nc.sbuf_tensor(name, list(shape), dt))

    def _emit_waits(self, eng, need):
        kn = self.known[eng]
        h = self.engs[eng]
        for k, v in need.items():
            if kn.get(k, 0) >= v:
                continue
            assert self.cnt[k] >= v, f"wait on future signal {k} {v} > {self.cnt[k]}"
            h.wait_ge(self.sems[k], v)
            kn[k] = v
            self.n_wait += 1

    def _waits(self, eng, reads, writes):
        need = {}
        me = "e_" + eng

        def add(dep, same_ok):
            if dep is None:
                return
            k, v = dep
            if same_ok and k == me:
                return
            if need.get(k, 0) < v:
                need[k] = v
        for t in reads:
            add(t.w, False)
        for t in writes:
            add(t.w, True)
            for d in t.r:
                add(d, True)
        self._emit_waits(eng, need)

    def _record(self, dep, reads, writes):
        for t in reads:
            if len(t.r) > 24:
                best = {}
                for k, v in t.r:
                    if best.get(k, 0) < v:
                        best[k] = v
                t.r = list(best.items())
            t.r.append(dep)
        for t in writes:
            t.w = dep
            t.r = []

    def op(self, eng, fn, reads=(), writes=(), sig=True):
        self._waits(eng, reads, writes)
        ins = fn(self.engs[eng])
        self.n_ins += 1
        k = "e_" + eng
        if sig:
            ins.then_inc(self.sems[k], 1)
            self.cnt[k] += 1
            dep = (k, self.cnt[k])
        else:
            dep = (k, self.cnt[k] + 1)
        self._record(dep, reads, writes)
        return dep

    def dma(self, eng, out, in_, semkey, reads=(), writes=(), **kw):
        if semkey not in self.sems:
            self._mksem(semkey)
        self._waits(eng, reads, writes)
        ins = self.engs[eng].dma_start(out=out, in_=in_, **kw)
        ins.then_inc(self.sems[semkey], 16)
        self.cnt[semkey] += 16
        dep = (semkey, self.cnt[semkey])
        self.n_ins += 1
        self._record(dep, reads, writes)
        return dep

    def barrier(self):
        for eng in self.engs:
            need = {k: v for k, v in self.cnt.items() if v > 0}
            self._emit_waits(eng, need)


def lam_init_of(L):
    return 0.8 - 0.6 * math.exp(-0.3 * L)


def host_consts():
    t = np.arange(T)
    row = (t // 64).astype(np.float32)
    col = (t % 64).astype(np.float32)
    n_freq = 16
    inv_freq = (np.float32(10000.0) ** (-np.arange(n_freq, dtype=np.float32) / np.float32(n_freq))).astype(np.float32)
    ang = np.concatenate([row[:, None] * inv_freq, col[:, None] * inv_freq], axis=-1).astype(np.float32)
    cos = np.cos(ang).astype(np.float32)
    sin = np.sin(ang).astype(np.float32)
    pidx = (np.arange(128) % 64) % 32
    rope = np.stack([cos[:, pidx].T, sin[:, pidx].T], 0).astype(np.float32)
    rmat = np.zeros((128, 128), np.float32)
    for m in range(128):
        if (m % 64) < 32:
            rmat[m + 32, m] = -1.0
        else:
            rmat[m - 32, m] = 1.0
    sl = np.arange(128)[:, None]
    u = np.arange(1152)[None, :]
    bandw = (np.abs(sl - (u - 512)) <= 128).astype(np.float32)
    return {"rope": np.ascontiguousarray(rope), "rmat": rmat, "bandw": np.ascontiguousarray(bandw)}


WEIGHT_SHAPES = {
    "adaln_w": (4, 1024, 6144), "adaln_b": (4, 6144), "norm1_g": (4, 1024), "norm2_g": (4, 1024),
    "ffn_w_up": (4, 1024, 5632), "ffn_conv_w": (4, 3, 5632), "ffn_conv_b": (4, 5632),
    "ffn_w_down": (4, 2816, 1024), "a_w_qkv": (2, 1024, 3072), "a_qk_g": (2, 2, 64),
    "a_lambda": (2, 4, 64), "a_head_g": (2, 128), "a_w_o": (2, 1024, 1024),
    "b_w_qkv": (1, 1024, 1536), "b_qk_g": (1, 2, 64), "b_w_o": (1, 1024, 1024),
    "c_w_qkv": (1, 1024, 1536), "c_qk_g": (1, 2, 64), "c_sink": (1, 16), "c_w_o": (1, 1024, 1024),
}


class _Stop(Exception):
    pass


def build_nc(n_layers=DEPTH, stop_after=None, dbg=""):
    holder = {}
    try:
        _build_inner(holder, n_layers, stop_after, dbg)
    except _Stop:
        pass
    return holder['nc']


def _build_inner(holder, n_layers, stop_after, dbg=""):
    nc = bass.Bass("TRN2", target_bir_lowering=False)
    holder['nc'] = nc

    def din(name, shape):
        return nc.dram_tensor(name, list(shape), F32, kind="ExternalInput").ap()

    x_d = din("x", [NB, T, D])
    c_d = din("c", [NB, D])
    ctx_d = din("ctx", [NB, C, D])
    cctx_d = din("c_ctx", [D])
    W = {k: din(k, s) for k, s in WEIGHT_SHAPES.items()}
    rope_d = din("rope", [2, 128, T])
    rmat_d = din("rmat", [128, 128])
    bandw_d = din("bandw", [128, 1152])
    out_d = nc.dram_tensor("out", [NB, T, D], F32, kind="ExternalOutput").ap()
    XT = nc.dram_tensor("xt_scratch", [NB, 128, KC, NT], F32, kind="Internal").ap()

    with ExitStack() as es:
        p = Prog(nc, es)
        PS = es.enter_context(nc.psum_tensor("ps", [128, 8, 512], F32))
        tPS = [Trk() for _ in range(8)]
        R1 = p.sbuf("r1", [128, KC, NT], BF16)
        R23 = p.sbuf("r23", [128, 2, KC, NT], BF16)
        QT = R23[:, 0]
        KT = R23[:, 1]
        Y = R23[:].rearrange("p a k t -> p (a k t)").bitcast(F32).rearrange("p (k t) -> p k t", k=KC)
        R4 = p.sbuf("r4", [128, 18944], BF16)
        R4f = R4[:].bitcast(F32)
        WA = p.sbuf("wa", [128, 12288], BF16)
        tWA = [Trk() for _ in range(6)]
        TM = p.sbuf("tm", [128, 10, 512], F32)
        tTM = [Trk() for _ in range(10)]
        TMflat = TM[:].rearrange("p a n -> p (a n)")
        FM = p.sbuf("fm", [128, 1024], F32)
        tFM = Trk()
        MOD = p.sbuf("mod", [128, 4, 3, 48], F32)
        AA = p.sbuf("aa", [128, 4, 3, 2, 8], F32)
        tMOD = Trk()
        IDENT = p.sbuf("ident_sb", [128, 128], F32)
        CB = p.sbuf("cb", [128, 5, 128], BF16)
        BANDW = p.sbuf("bandw_sb", [128, 1152], BF16)
        SCT = p.sbuf("sct", [128, KC, 3], BF16)
        ONESF = p.sbuf("onesf", [128, 128], F32)
        attn_it = [0]
        SM = p.sbuf("sm", [128, 16], F32)
        LAMB = p.sbuf("lamb", [128, 2, 256], F32)
        LTMP = p.sbuf("ltmp", [128, 2, 2, 64], F32)
        LS = p.sbuf("ls", [128, 2, 2], F32)
        tC = Trk()
        tH = [Trk() for _ in range(5)]
        tQ = [[Trk() for _ in range(5)] for _ in range(KC)]
        tK = [[Trk() for _ in range(5)] for _ in range(KC)]
        tV = [Trk() for _ in range(NTILE)]
        tY = [Trk() for _ in range(5)]
        tXT = [Trk() for _ in range(5)]
        tR4 = [Trk() for _ in range(4)]
        tACT = [Trk(), Trk()]

        def bank(i):
            return PS[:, i, :]

        def tmf(i, n=512):
            return TM[:, i, 0:n]

        def tmb(i, half, n=512):
            return TM[:, i, :].bitcast(BF16)[:, half * 512: half * 512 + n]

        ones_mean = CB[:, 0, :]
        blockdiag = CB[:, 1, :]
        ones128m = CB[:, 2, :]
        ones1 = CB[:, 3, :]
        rmat = CB[:, 4, :]
        epsc = SM[:, 0:1]

        def ckpt(i):
            if stop_after is not None and stop_after == i:
                raise _Stop()

        p.op("pool", lambda e: e.memset(IDENT[:], 0.0), writes=[tC])
        p.op("pool", lambda e: e.affine_select(IDENT[:], IDENT[:], pattern=[[-1, 128]], compare_op=ALU.not_equal,
                                               fill=1.0, base=0, channel_multiplier=1), reads=[tC], writes=[tC])
        p.op("dve", lambda e: e.memset(CB[:, 0, :], 1.0 / 1024.0), writes=[tC])
        p.op("dve", lambda e: e.memset(CB[:, 1, :], 0.0), writes=[tC])
        p.op("dve", lambda e: e.memset(CB[0:64, 1, 0:64], 1.0 / 64.0), writes=[tC])
        p.op("dve", lambda e: e.memset(CB[64:128, 1, 64:128], 1.0 / 64.0), writes=[tC])
        p.op("dve", lambda e: e.memset(CB[:, 2, :], 1.0 / 128.0), writes=[tC])
        p.op("dve", lambda e: e.memset(CB[:, 3, :], 1.0), writes=[tC])
        p.op("dve", lambda e: e.memset(ONESF[:], 1.0), writes=[tC])
        p.op("dve", lambda e: e.memset(SM[:], 0.0), writes=[tC])
        p.op("dve", lambda e: e.memset(SM[:, 0:1], EPS), writes=[tC])
        p.dma("pool", CB[:, 4, :], rmat_d, "d_c1", writes=[tC])
        p.dma("pool", BANDW[:], bandw_d, "d_c2", writes=[tC])

        def stg(j):
            return TM[:, j // 4, (j % 4) * 128:(j % 4) * 128 + 128]
        tST = Trk()
        p.op("pool", lambda e: e.memset(TMflat[:, 0:4096], 0.0), writes=[tST])
        rowpos = [0]

        def add_rows(src2d, nrows):
            r0 = rowpos[0]
            done = 0
            while done < nrows:
                r = r0 + done
                j, off = r // 128, r % 128
                n = min(128 - off, nrows - done)
                p.dma("sp", stg(j)[off:off + n, :], src2d[done:done + n, :], "d_rows", writes=[tST])
                done += n
            rowpos[0] += nrows
            return r0
        COL_C = add_rows(c_d.rearrange("b (k q) -> (b k) q", q=128), 16)
        add_rows(cctx_d.rearrange("(k q) -> k q", q=128), 8)
        COL_AB = add_rows(W["adaln_b"].rearrange("l (m q) -> (l m) q", q=128), 192)
        COL_N1 = add_rows(W["norm1_g"].rearrange("l (m q) -> (l m) q", q=128), 32)
        COL_N2 = add_rows(W["norm2_g"].rearrange("l (m q) -> (l m) q", q=128), 32)
        COL_CW = add_rows(W["ffn_conv_w"].rearrange("l j (m q) -> (l j m) q", q=128), 528)
        COL_CBI = add_rows(W["ffn_conv_b"].rearrange("l (m q) -> (l m) q", q=128), 176)
        COL_HG = add_rows(W["a_head_g"], 2)
        COL_QKG = rowpos[0]
        for nm, nl in (("a_qk_g", 2), ("b_qk_g", 1), ("c_qk_g", 1)):
            src = W[nm].rearrange("l a d -> (l a) d")
            for r in range(nl * 2):
                rr = rowpos[0]
                j, off = rr // 128, rr % 128
                for hh in range(2):
                    p.dma("sp", stg(j)[off:off + 1, hh * 64:(hh + 1) * 64], src[r:r + 1, :], "d_rows", writes=[tST])
                rowpos[0] += 1
        assert rowpos[0] <= 1024
        QKG_COL = {("a", 0): COL_QKG, ("a", 1): COL_QKG + 2, ("b", 0): COL_QKG + 4, ("c", 0): COL_QKG + 6}
        for j in range(8):
            p.op("pe", lambda e, j=j: e.transpose(bank(j % 2)[:, 0:128], stg(j), IDENT[:]), reads=[tST, tC], writes=[tPS[j % 2]])
            p.op("dve", lambda e, j=j: e.tensor_copy(FM[:, j * 128:(j + 1) * 128], bank(j % 2)[:, 0:128]), reads=[tPS[j % 2]], writes=[tFM])
        p.op("act", lambda e: e.activation(SCT[:].rearrange("p k s -> p s k"), FM[:, 0:24].rearrange("p (s k) -> p s k", s=3), AF.Silu),
             reads=[tFM], writes=[tC])

        for j in range(2):
            L_ = 3 * j
            p.dma("sp", LAMB[:, j, :], W["a_lambda"][j].rearrange("a d -> (a d)").partition_broadcast(128), "d_lam", writes=[tC])
        for j in range(2):
            L_ = 3 * j
            lv = LAMB[:, j, :].rearrange("p (a b d) -> p a b d", a=2, b=2)
            p.op("dve", lambda e, lv=lv: e.tensor_tensor(LTMP[:, :, 0, :], lv[:, :, 0, :], lv[:, :, 1, :], ALU.mult), reads=[tC], writes=[tC])
            p.op("dve", lambda e: e.reduce_sum(LS[:, :, 0], LTMP[:, :, 0, :], axis=AX.X), reads=[tC], writes=[tC])
            p.op("act", lambda e: e.activation(LS[:, :, 1], LS[:, :, 0], AF.Exp), reads=[tC], writes=[tC])
            p.op("dve", lambda e, j=j: e.tensor_tensor(SM[:, 1 + j:2 + j], LS[:, 1, 1:2], LS[:, 0, 1:2], ALU.subtract), reads=[tC], writes=[tC])
            p.op("dve", lambda e, j=j, L_=L_: e.tensor_scalar(SM[:, 1 + j:2 + j], SM[:, 1 + j:2 + j], -lam_init_of(L_), None, ALU.add), reads=[tC], writes=[tC])
            p.op("dve", lambda e, j=j, L_=L_: e.tensor_scalar(SM[:, 3 + j:4 + j], FM[:, COL_HG + j:COL_HG + j + 1], 1.0 - lam_init_of(L_), None, ALU.mult),
                 reads=[tC, tFM], writes=[tC])
        for h in range(16):
            p.dma("sp", SM[(h % 2) * 64:(h % 2) * 64 + 64, 8 + h // 2:9 + h // 2], W["c_sink"][0][h:h + 1].partition_broadcast(64), "d_lam", writes=[tC])
        p.op("act", lambda e: e.activation(SM[:, 8:16], SM[:, 8:16], AF.Exp), reads=[tC], writes=[tC])

        ckpt(0)
        p.barrier()
        nload = [0]
        for L in range(n_layers):
            wv = W["adaln_w"][L].rearrange("(c q) n -> q c n", q=128)
            pb = L % 2
            for g in range(8):
                buf = nload[0] % 2
                nload[0] += 1
                wt = WA[:, buf * 6144:(buf + 1) * 6144].rearrange("p (c n) -> p c n", c=KC)
                trs = tWA[buf * 3:buf * 3 + 3]
                p.dma("pool", wt, wv[:, :, g * 768:(g + 1) * 768], f"d_w{buf}", writes=trs)
                for mm in range(6):
                    m = g * 6 + mm
                    for k in range(KC):
                        p.op("pe", lambda e, k=k, m=m, mm=mm, wt=wt, pb=pb: e.matmul(bank(pb)[:, m * 3:m * 3 + 3], lhsT=wt[:, k, mm * 128:(mm + 1) * 128],
                                                                             rhs=SCT[:, k, :], start=(k == 0), stop=(k == KC - 1)),
                             reads=trs + [tC], writes=[tPS[pb]], sig=(k == KC - 1))
            for s in range(3):
                src = bank(pb)[:, 0:144].rearrange("p (m s) -> p s m", s=3)[:, s, :]
                p.op("dve", lambda e, s=s, src=src, L=L: e.tensor_tensor(MOD[:, L, s, :], src, FM[:, COL_AB + L * 48:COL_AB + (L + 1) * 48], ALU.add),
                     reads=[tPS[pb], tFM], writes=[tMOD])
                for w_, (mo, gc) in enumerate(((8, COL_N1), (32, COL_N2))):
                    p.op("dve", lambda e, s=s, L=L, w_=w_, mo=mo, gc=gc: e.scalar_tensor_tensor(AA[:, L, s, w_, :], MOD[:, L, s, mo:mo + 8], 1.0,
                                                                                           FM[:, gc + L * 8:gc + L * 8 + 8], ALU.add, ALU.mult),
                         reads=[tMOD, tFM], writes=[tMOD])
        p.barrier()

        ckpt(1)

        def Acol(L, s, w_, k):
            return AA[:, L, s, w_, k:k + 1]

        def Bcol(L, s, w_, k):
            return MOD[:, L, s, (0 if w_ == 0 else 24) + k:(0 if w_ == 0 else 24) + k + 1]

        def Gcol(L, s, w_, k):
            return MOD[:, L, s, (16 if w_ == 0 else 40) + k:(16 if w_ == 0 else 40) + k + 1]

        def norm_chunk(L, s, w_, tc):
            t0, n = TCH[tc]
            pb = 6 + (tc % 2)
            for k in range(KC):
                sq = tmb(0, k % 2, n)
                p.op("act", lambda e, k=k, sq=sq: e.activation(sq, Y[:, k, t0:t0 + n], AF.Square), reads=[tY[tc]], writes=[tTM[0]])
                p.op("pe", lambda e, k=k, sq=sq: e.matmul(bank(pb)[:, 0:n], lhsT=ones_mean, rhs=sq, start=(k == 0), stop=(k == KC - 1)),
                     reads=[tTM[0], tC], writes=[tPS[pb]], sig=True)
            p.op("act", lambda e: e.activation(tmf(1, n), bank(pb)[:, 0:n], AF.Ln, bias=epsc, scale=1.0), reads=[tPS[pb], tC], writes=[tTM[1]])
            p.op("act", lambda e: e.activation(tmf(2, n), tmf(1, n), AF.Exp, scale=-0.5), reads=[tTM[1]], writes=[tTM[2]])
            for k in range(KC):
                sl = 3 + (k % 3)
                p.op("dve", lambda e, k=k, sl=sl: e.scalar_tensor_tensor(tmf(sl, n), Y[:, k, t0:t0 + n], Acol(L, s, w_, k), tmf(2, n), ALU.mult, ALU.mult),
                     reads=[tY[tc], tTM[2], tMOD], writes=[tTM[sl]])
                p.op("act", lambda e, k=k, sl=sl: e.activation(R1[:, k, t0:t0 + n], tmf(sl, n), AF.Identity, bias=Bcol(L, s, w_, k), scale=1.0),
                     reads=[tTM[sl], tMOD], writes=[tH[tc]])

        def load_x(b):
            for tc in range(5):
                t0, n = TCH[tc]
                nt = n // 128
                sb = tc % 2
                xs = R4f[:, sb * 4096:(sb + 1) * 4096].rearrange("p (i f) -> p i f", f=D)
                src = x_d[b][t0:t0 + n] if tc < 4 else ctx_d[b]
                p.dma("sp", xs[:, 0:nt, :], src.rearrange("(i q) f -> q i f", q=128), f"d_xs{sb}", writes=[tR4[sb]])
                for k in range(KC):
                    pb = k % 4
                    for i in range(nt):
                        p.op("pe", lambda e, k=k, i=i, pb=pb, xs=xs: e.transpose(bank(pb)[:, i * 128:(i + 1) * 128], xs[:, i, k * 128:(k + 1) * 128], IDENT[:]),
                             reads=[tR4[sb], tC], writes=[tPS[pb]], sig=(i == nt - 1))
                    if k % 2 == 0:
                        p.op("dve", lambda e, k=k, pb=pb: e.tensor_copy(Y[:, k, t0:t0 + n], bank(pb)[:, 0:n]), reads=[tPS[pb]], writes=[tY[tc]])
                    else:
                        p.op("act", lambda e, k=k, pb=pb: e.copy(Y[:, k, t0:t0 + n], bank(pb)[:, 0:n]), reads=[tPS[pb]], writes=[tY[tc]])
                p.dma("sp", XT[b][:, :, t0:t0 + n], Y[:, :, t0:t0 + n], f"d_xst{tc}", reads=[tY[tc]], writes=[tXT[tc]])

        def store_out(b):
            for tc in range(4):
                t0, n = TCH[tc]
                sb = tc % 2
                os_ = R4f[:, sb * 4096:(sb + 1) * 4096].rearrange("p (i f) -> p i f", f=D)
                cnt = 0
                for i in range(4):
                    for kg in range(2):
                        pb = cnt % 4
                        cnt += 1
                        for kk in range(4):
                            k = kg * 4 + kk
                            p.op("pe", lambda e, k=k, kk=kk, i=i, pb=pb: e.transpose(bank(pb)[:, kk * 128:(kk + 1) * 128], Y[:, k, t0 + i * 128:t0 + (i + 1) * 128], IDENT[:]),
                                 reads=[tY[tc], tC], writes=[tPS[pb]], sig=(kk == 3))
                        if cnt % 2 == 0:
                            p.op("dve", lambda e, i=i, kg=kg, pb=pb, os_=os_: e.tensor_copy(os_[:, i, kg * 512:(kg + 1) * 512], bank(pb)), reads=[tPS[pb]], writes=[tR4[sb]])
                        else:
                            p.op("act", lambda e, i=i, kg=kg, pb=pb, os_=os_: e.copy(os_[:, i, kg * 512:(kg + 1) * 512], bank(pb)), reads=[tPS[pb]], writes=[tR4[sb]])
                p.dma("sp", out_d[b][t0:t0 + n].rearrange("(i q) f -> q i f", q=128), os_, f"d_out{sb}", reads=[tR4[sb]], writes=[tXT[tc]])

        qkset = [0]

        def qk_post(pb, n, dest, gcol, rope_t0, tdest):
            st = qkset[0] % 2
            qkset[0] += 1
            sa, sb_, sc_ = 3 * st, 3 * st + 1, 3 * st + 2
            msb, rotb = 4 + st, 6 + st
            sq = tmb(sa, 0, n)
            p.op("act", lambda e: e.activation(sq, bank(pb)[:, 0:n], AF.Square), reads=[tPS[pb]], writes=[tTM[sa]])
            p.op("pe", lambda e: e.matmul(bank(msb)[:, 0:n], lhsT=blockdiag, rhs=sq, start=True, stop=True), reads=[tTM[sa], tC], writes=[tPS[msb]])
            p.op("act", lambda e: e.activation(tmf(sb_, n), bank(msb)[:, 0:n], AF.Ln, bias=epsc, scale=1.0), reads=[tPS[msb], tC], writes=[tTM[sb_]])
            p.op("act", lambda e: e.activation(tmf(sb_, n), tmf(sb_, n), AF.Exp, scale=-0.5), reads=[tTM[sb_]], writes=[tTM[sb_]])
            g = FM[:, gcol:gcol + 1]
            if rope_t0 is None:
                p.op("dve", lambda e: e.scalar_tensor_tensor(dest, bank(pb)[:, 0:n], g, tmf(sb_, n), ALU.mult, ALU.mult),
                     reads=[tPS[pb], tTM[sb_], tFM], writes=[tdest])
                return
            qn = tmb(sa, 1, n)
            p.op("dve", lambda e: e.scalar_tensor_tensor(qn, bank(pb)[:, 0:n], g, tmf(sb_, n), ALU.mult, ALU.mult),
                 reads=[tPS[pb], tTM[sb_], tFM], writes=[tTM[sa]])
            p.op("pe", lambda e: e.matmul(bank(rotb)[:, 0:n], lhsT=rmat, rhs=qn, start=True, stop=True), reads=[tTM[sa], tC], writes=[tPS[rotb]])
            cosv = TMflat[:, 6 * 512:8 * 512].bitcast(BF16)[:, rope_t0:rope_t0 + n]
            sinv = TMflat[:, 8 * 512:10 * 512].bitcast(BF16)[:, rope_t0:rope_t0 + n]
            p.op("dve", lambda e: e.tensor_tensor(tmf(sc_, n), qn, cosv, ALU.mult), reads=[tTM[sa], tTM[6]], writes=[tTM[sc_]])
            p.op("dve", lambda e: e.tensor_tensor(tmf(sb_, n), bank(rotb)[:, 0:n], sinv, ALU.mult), reads=[tPS[rotb], tTM[6], tTM[sb_]], writes=[tTM[sb_]])
            p.op("pool", lambda e: e.tensor_tensor(dest, tmf(sc_, n), tmf(sb_, n), ALU.add), reads=[tTM[sc_], tTM[sb_]], writes=[tdest])

        def qkv_setup(L, kind, jj):
            nm = "abc"[kind]
            wq = W[nm + "_w_qkv"][jj].rearrange("(c q) n -> q c n", q=128)
            if kind == 0:
                groups = [("q", 0, 0), ("q", 512, 4), ("k", 1024, 0), ("k", 1536, 4), ("v", 2048, 0), ("v", 2560, 1)]
            else:
                groups = [("q", 0, 0), ("q", 512, 4), ("kd", 1024, 0), ("vg", 1280, 0)]

            def slot(i):
                return WA[:, (i % 3) * 4096:(i % 3 + 1) * 4096].rearrange("p (c n) -> p c n", c=KC), tWA[(i % 3) * 2:(i % 3) * 2 + 2]

            def load(i):
                ty, co, _ = groups[i]
                wt, trs = slot(i)
                if ty == "kd":
                    wt4 = wt.rearrange("p c (g d) -> p c g d", g=4)
                    for g_ in range(4):
                        for hh in range(2):
                            p.dma("pool", wt4[:, :, g_, hh * 64:(hh + 1) * 64], wq[:, :, co + g_ * 64:co + (g_ + 1) * 64], f"d_wq{i % 3}", writes=trs)
                elif ty == "vg":
                    p.dma("pool", wt[:, :, 0:256], wq[:, :, co:co + 256], f"d_wq{i % 3}", writes=trs)
                else:
                    p.dma("pool", wt, wq[:, :, co:co + 512], f"d_wq{i % 3}", writes=trs)
            return groups, slot, load

        def qkv_prefetch(L, kind, jj):
            groups, slot, load = qkv_setup(L, kind, jj)
            p.dma("pool", TMflat[:, 6 * 512:8 * 512].bitcast(BF16), rope_d[0], "d_rope0", writes=[tTM[6]])
            p.dma("pool", TMflat[:, 8 * 512:10 * 512].bitcast(BF16), rope_d[1], "d_rope1", writes=[tTM[6]])
            load(0)
            load(1)

        def qkv_phase(L, kind, jj, need_ctx):
            nm = "abc"[kind]
            gq = QKG_COL[(nm, 0)] + 2 * jj * (1 if nm == "a" else 0)
            gk = gq + 1
            groups, slot, load = qkv_setup(L, kind, jj)
            ng = len(groups)
            pbc = [0]
            items = []
            for gi in range(ng):
                ty, co, cb = groups[gi]
                if ty not in ("q", "k", "kd"):
                    continue
                for cc in range(4):
                    for tc in range(5):
                        if ty == "q" and tc == 4 and not need_ctx:
                            continue
                        items.append((gi, cc, tc, (cc == 0 and tc == 0)))
            state = {}

            def emit_main(i):
                gi, cc, tc, first = items[i]
                ty, co, cb = groups[gi]
                if first and gi + 2 < ng:
                    load(gi + 2)
                wt, trs = slot(gi)
                t0, n = TCH[tc]
                pb = pbc[0] % 4
                pbc[0] += 1
                state[i] = pb
                for k in range(KC):
                    p.op("pe", lambda e, k=k, cc=cc, pb=pb, wt=wt, t0=t0, n=n: e.matmul(bank(pb)[:, 0:n], lhsT=wt[:, k, cc * 128:(cc + 1) * 128],
                                                                                     rhs=R1[:, k, t0:t0 + n], start=(k == 0), stop=(k == KC - 1)),
                         reads=trs + [tH[tc]], writes=[tPS[pb]], sig=(k == KC - 1))

            def emit_post(i):
                gi, cc, tc, first = items[i]
                ty, co, cb = groups[gi]
                t0, n = TCH[tc]
                cdst = cb + cc
                pb = state[i]
                if ty == "q":
                    qk_post(pb, n, QT[:, cdst, t0:t0 + n], gq, t0 if tc < 4 else None, tQ[cdst][tc])
                else:
                    qk_post(pb, n, KT[:, cdst, t0:t0 + n], gk, t0 if tc < 4 else None, tK[cdst][tc])
            nxt = 0
            for i in range(len(items)):
                while nxt <= min(i + 2, len(items) - 1):
                    emit_main(nxt)
                    nxt += 1
                emit_post(i)
            for gi in range(ng):
                ty, co, cb = groups[gi]
                if ty in ("q", "k", "kd"):
                    continue
                if gi + 2 < ng:
                    load(gi + 2)
                wt, trs = slot(gi)
                ncol = 512 if ty == "v" else 256
                vw = 1024 if kind == 0 else 256
                Vv = R4[:, 0:NTILE * vw].rearrange("p (s e) -> p s e", e=vw)
                for ti in range(NTILE):
                    tc = min(ti // 4, 4)
                    pb = pbc[0] % 4
                    pbc[0] += 1
                    for k in range(KC):
                        p.op("pe", lambda e, k=k, pb=pb, ti=ti, wt=wt, ncol=ncol: e.matmul(bank(pb)[:, 0:ncol], lhsT=R1[:, k, ti * 128:(ti + 1) * 128],
                                                                                       rhs=wt[:, k, 0:ncol], start=(k == 0), stop=(k == KC - 1)),
                             reads=trs + [tH[tc]], writes=[tPS[pb]], sig=(k == KC - 1))
                    dst = Vv[:, ti, cb * 512:cb * 512 + ncol]
                    if ti % 2 == 0:
                        p.op("dve", lambda e, pb=pb, dst=dst, ncol=ncol: e.tensor_copy(dst, bank(pb)[:, 0:ncol]), reads=[tPS[pb]], writes=[tV[ti]])
                    else:
                        p.op("act", lambda e, pb=pb, dst=dst, ncol=ncol: e.copy(dst, bank(pb)[:, 0:ncol]), reads=[tPS[pb]], writes=[tV[ti]])

        def attention(L, kind, jj, q0, nq, tcq, stiles):
            vw = 1024 if kind == 0 else 256
            Vv = R4[:, 0:NTILE * vw].rearrange("p (s e) -> p s e", e=vw)
            ns = len(stiles)
            for c in range(KC):
                kc = c if kind == 0 else c // 2
                nring = 3 if kind == 0 else 2
                if kind == 0:
                    accb = [6, 7]
                    sumb = 2 * (ns % 3)
                else:
                    ab0 = 4 + 2 * (c % 2)
                    accb = [ab0, ab0 + 1]

                aslot = 6 + 2 * (attn_it[0] % 2)
                attn_it[0] += 1
                acc32 = TMflat[:, aslot * 512:(aslot + 2) * 512].rearrange("p (h n) -> p h n", h=2)

                def S(i):
                    ti, _ = stiles[i]
                    sb = 2 * (i % nring)
                    tck = min(ti // 4, 4)
                    for hh in range(2):
                        p.op("pe", lambda e, hh=hh, ti=ti, sb=sb: e.matmul(bank(sb + hh)[:, 0:nq], lhsT=KT[hh * 64:(hh + 1) * 64, kc, ti * 128:(ti + 1) * 128],
                                                                      rhs=QT[hh * 64:(hh + 1) * 64, c, q0:q0 + nq], start=True, stop=True),
                             reads=[tK[kc][tck], tQ[c][tcq]], writes=[tPS[sb], tPS[sb + 1]], sig=(hh == 1))

                def E(i):
                    ti, mo = stiles[i]
                    sb = 2 * (i % nring)
                    sl = i % 3
                    pt = TM[:, sl, :].bitcast(BF16).rearrange("p (h n) -> p h n", h=2)
                    p.op("act", lambda e, sb=sb, pt=pt: e.activation(pt[:, :, 0:nq], PS[:, sb:sb + 2, 0:nq], AF.Exp, scale=0.125),
                         reads=[tPS[sb], tPS[sb + 1]], writes=[tTM[sl]])
                    if mo is not None:
                        mk = BANDW[:, mo:mo + nq].unsqueeze(1).broadcast_to([128, 2, nq])
                        p.op("dve", lambda e, pt=pt, mk=mk: e.tensor_tensor(pt[:, :, 0:nq], pt[:, :, 0:nq], mk, ALU.mult),
                             reads=[tTM[sl], tC], writes=[tTM[sl]])

                def PV(i):
                    ti, _ = stiles[i]
                    sl = i % 3
                    pt = TM[:, sl, :].bitcast(BF16).rearrange("p (h n) -> p h n", h=2)
                    st, sp_ = (i == 0), (i == ns - 1)
                    wr = [tPS[b_] for b_ in accb]
                    if kind == 0:
                        vh = Vv[:, ti, c * 128:(c + 1) * 128]
                        if i == 0:
                            p.op("dve", lambda e, pt=pt: e.tensor_copy(acc32[:, :, 0:nq], pt[:, :, 0:nq]), reads=[tTM[sl]], writes=[tTM[aslot]])
                        else:
                            p.op("dve", lambda e, pt=pt: e.tensor_tensor(acc32[:, :, 0:nq], acc32[:, :, 0:nq], pt[:, :, 0:nq], ALU.add), reads=[tTM[sl], tTM[aslot]], writes=[tTM[aslot]])
                        for hh in range(2):
                            p.op("pe", lambda e, hh=hh, vh=vh, pt=pt: e.matmul(bank(accb[hh])[:, 0:nq], lhsT=vh, rhs=pt[:, hh, 0:nq], start=st, stop=sp_),
                                 reads=[tV[ti], tTM[sl]], writes=wr, sig=(hh == 1))
                        if sp_:
                            for hh in range(2):
                                p.op("pe", lambda e, hh=hh: e.matmul(bank(sumb + hh)[:, 0:nq], lhsT=ONESF[:], rhs=acc32[:, hh, 0:nq], start=True, stop=True),
                                     reads=[tC, tTM[aslot]], writes=[tPS[sumb], tPS[sumb + 1]], sig=(hh == 1))
                    else:
                        vg = Vv[:, ti, kc * 64:(kc + 1) * 64]
                        for hh in range(2):
                            p.op("pe", lambda e, hh=hh, vg=vg, pt=pt: e.matmul(bank(accb[0])[hh * 64:(hh + 1) * 64, 0:nq], lhsT=vg, rhs=pt[:, hh, 0:nq], start=st, stop=sp_,
                                                                         tile_position=(0, hh * 64)),
                                 reads=[tV[ti], tTM[sl]], writes=wr, sig=False)
                        for hh in range(2):
                            p.op("pe", lambda e, hh=hh, pt=pt: e.matmul(bank(accb[1])[hh * 64:(hh + 1) * 64, 0:nq], lhsT=ones1[:, 0:64], rhs=pt[:, hh, 0:nq], start=st, stop=sp_,
                                                                  tile_position=(0, hh * 64)),
                                 reads=[tC, tTM[sl]], writes=wr, sig=(hh == 1))

                for i0 in range(min(nring - 1, ns)):
                    S(i0)
                for i in range(ns):
                    if i + nring - 1 < ns:
                        S(i + nring - 1)
                    E(i)
                    PV(i)
                rd = [tPS[b_] for b_ in accb]
                dst = R1[:, c, q0:q0 + nq]
                if kind == 0:
                    rs_ = [tPS[sumb], tPS[sumb + 1]]
                    p.op("act", lambda e: e.activation(tmf(3, nq), bank(sumb)[:, 0:nq], AF.Ln), reads=rs_, writes=[tTM[3]])
                    p.op("act", lambda e: e.activation(tmf(4, nq), bank(sumb + 1)[:, 0:nq], AF.Ln), reads=rs_, writes=[tTM[4]])
                    p.op("act", lambda e: e.activation(tmf(3, nq), tmf(3, nq), AF.Exp, scale=-1.0), reads=[tTM[3]], writes=[tTM[3]])
                    p.op("act", lambda e: e.activation(tmf(4, nq), tmf(4, nq), AF.Exp, scale=-1.0), reads=[tTM[4]], writes=[tTM[4]])
                    p.op("dve", lambda e: e.tensor_tensor(tmf(3, nq), bank(accb[0])[:, 0:nq], tmf(3, nq), ALU.mult), reads=rd + [tTM[3]], writes=[tTM[3]])
                    p.op("dve", lambda e: e.tensor_tensor(tmf(4, nq), bank(accb[1])[:, 0:nq], tmf(4, nq), ALU.mult), reads=rd + [tTM[4]], writes=[tTM[4]])
                    p.op("dve", lambda e: e.scalar_tensor_tensor(tmf(4, nq), tmf(4, nq), SM[:, 1 + jj:2 + jj], tmf(3, nq), ALU.mult, ALU.add),
                         reads=[tTM[3], tTM[4], tC], writes=[tTM[4]])
                    sq = tmb(5, 0, nq)
                    p.op("act", lambda e: e.activation(sq, tmf(4, nq), AF.Square), reads=[tTM[4]], writes=[tTM[5]])
                    p.op("pe", lambda e: e.matmul(bank(accb[0])[:, 0:nq], lhsT=ones128m, rhs=sq, start=True, stop=True), reads=[tTM[5], tC], writes=rd)
                    p.op("act", lambda e: e.activation(tmf(3, nq), bank(accb[0])[:, 0:nq], AF.Ln, bias=epsc, scale=1.0), reads=rd + [tC, tTM[3]], writes=[tTM[3]])
                    p.op("act", lambda e: e.activation(tmf(3, nq), tmf(3, nq), AF.Exp, scale=-0.5), reads=[tTM[3]], writes=[tTM[3]])
                    p.op("dve", lambda e: e.scalar_tensor_tensor(dst, tmf(4, nq), SM[:, 3 + jj:4 + jj], tmf(3, nq), ALU.mult, ALU.mult),
                         reads=[tTM[4], tTM[3], tC], writes=[tH[tcq]])
                else:
                    if kind == 2:
                        p.op("act", lambda e: e.activation(tmf(3, nq), bank(accb[1])[:, 0:nq], AF.Ln, bias=SM[:, 8 + c:9 + c], scale=1.0), reads=rd + [tC], writes=[tTM[3]])
                    else:
                        p.op("act", lambda e: e.activation(tmf(3, nq), bank(accb[1])[:, 0:nq], AF.Ln), reads=rd, writes=[tTM[3]])
                    p.op("act", lambda e: e.activation(tmf(3, nq), tmf(3, nq), AF.Exp, scale=-1.0), reads=[tTM[3]], writes=[tTM[3]])
                    p.op("dve", lambda e: e.tensor_tensor(dst, bank(accb[0])[:, 0:nq], tmf(3, nq), ALU.mult), reads=rd + [tTM[3]], writes=[tH[tcq]])

        def outproj_prefetch(kind, jj):
            nm = "abc"[kind]
            wo = W[nm + "_w_o"][jj].rearrange("(c q) n -> q c n", q=128)
            WO = WA[:, 4096:12288].rearrange("p (c n) -> p c n", c=KC)
            p.dma("pool", WO, wo, "d_wo", writes=tWA[2:6])

        def outproj_phase(b, L, kind, jj, last):
            nm = "abc"[kind]
            wo = W[nm + "_w_o"][jj].rearrange("(c q) n -> q c n", q=128)
            WO = WA[:, 4096:12288].rearrange("p (c n) -> p c n", c=KC)
            trs = tWA[2:6]
            pbc = 0
            for tc in range(4 if last else 5):
                t0, n = TCH[tc]
                s = b if tc < 4 else 2
                p.dma("sp", Y[:, :, t0:t0 + n], XT[b][:, :, t0:t0 + n], f"d_xld{tc}", reads=[tXT[tc]], writes=[tY[tc]])
                for m in range(KC if "noattn" not in dbg else 0):
                    pb = pbc % 6
                    pbc += 1
                    for k in range(KC):
                        p.op("pe", lambda e, k=k, m=m, pb=pb: e.matmul(bank(pb)[:, 0:n], lhsT=WO[:, k, m * 128:(m + 1) * 128], rhs=R1[:, k, t0:t0 + n],
                                                                   start=(k == 0), stop=(k == KC - 1)),
                             reads=trs + [tH[tc]], writes=[tPS[pb]], sig=(k == KC - 1))
                    p.op("dve", lambda e, m=m, pb=pb: e.scalar_tensor_tensor(Y[:, m, t0:t0 + n], bank(pb)[:, 0:n], Gcol(L, s, 0, m), Y[:, m, t0:t0 + n], ALU.mult, ALU.add),
                         reads=[tPS[pb], tY[tc], tMOD], writes=[tY[tc]])
                norm_chunk(L, s, 1, tc)

        def ffn_phase(b, L, last):
            ntc = 4 if last else 5
            UG = R4f[:, 0:2310]
            UV = R4f[:, 2310:4620]
            CG = R4f[:, 4620:6930]
            CV = R4f[:, 6930:9240]
            Ub = (UG, UV)
            Cb = (CG, CV)
            tU = [[Trk() for _ in range(5)] for _ in range(2)]
            tCc = [[Trk() for _ in range(5)] for _ in range(2)]
            tAT = [[Trk() for _ in range(5)] for _ in range(2)]
            tPad = Trk()
            for U_ in Ub:
                for a0 in (0, 2050, 2308):
                    p.op("pool", lambda e, U_=U_, a0=a0: e.memset(U_[:, a0:a0 + 2], 0.0), writes=[tPad])
            wu = W["ffn_w_up"][L].rearrange("(c q) n -> q c n", q=128)
            wd = W["ffn_w_down"][L]

            def colof(tc):
                return 2 + 512 * tc if tc < 4 else 2052

            def bufs(gi):
                bsel = gi % 2
                base = bsel * 6144
                WU = WA[:, base:base + 4096].rearrange("p (c n) -> p c n", c=KC)
                WD = WA[:, base + 4096:base + 6144].rearrange("p (j n) -> p j n", j=2)
                return WU, WD, tWA[bsel * 3:bsel * 3 + 2], tWA[bsel * 3 + 2:bsel * 3 + 3], bsel

            def loadU(gi):
                WU, WD, trU, trD, bsel = bufs(gi)
                j0 = gi * 2
                p.dma("pool", WU[:, :, 0:256], wu[:, :, j0 * 128:j0 * 128 + 256], f"d_fu{bsel}", writes=trU)
                p.dma("pool", WU[:, :, 256:512], wu[:, :, DFF + j0 * 128:DFF + j0 * 128 + 256], f"d_fu{bsel}", writes=trU)

            def loadD(gi):
                WU, WD, trU, trD, bsel = bufs(gi)
                j0 = gi * 2
                p.dma("pool", WD, wd[j0 * 128:j0 * 128 + 256, :].rearrange("(j q) n -> q j n", q=128), f"d_fd{bsel}", writes=trD)

            def actt(gi):
                r = gi % 2
                return TMflat[:, r * 2560:(r + 1) * 2560].bitcast(BF16)[:, 0:4620].rearrange("p (j n) -> p j n", j=2)

            pbc = [0]
            dnc = [0]

            def up_mm(gi, jl, tc):
                WU, WD, trU, trD, bsel = bufs(gi)
                t0, n = TCH[tc]
                co = colof(tc)
                for which in range(2):
                    pb = pbc[0] % 6
                    pbc[0] += 1
                    wc = which * 256 + jl * 128
                    for k in range(KC):
                        p.op("pe", lambda e, k=k, pb=pb, wc=wc, WU=WU, t0=t0, n=n: e.matmul(bank(pb)[:, 0:n], lhsT=WU[:, k, wc:wc + 128], rhs=R1[:, k, t0:t0 + n],
                                                                                         start=(k == 0), stop=(k == KC - 1)),
                             reads=trU + [tH[tc]], writes=[tPS[pb]], sig=(k == KC - 1))
                    p.op("act", lambda e, pb=pb, which=which, co=co, n=n: e.copy(Ub[which][:, co:co + n], bank(pb)[:, 0:n]), reads=[tPS[pb]], writes=[tU[which][tc]])

            def conv(gi, jl, tc):
                j = gi * 2 + jl
                t0, n = TCH[tc]
                co = colof(tc)
                AT = actt(gi)
                for which in range(2):
                    U_, Cc = Ub[which], Cb[which]
                    f = which * FC + j
                    w0 = FM[:, COL_CW + (L * 3 + 0) * 44 + f:COL_CW + (L * 3 + 0) * 44 + f + 1]
                    w1 = FM[:, COL_CW + (L * 3 + 1) * 44 + f:COL_CW + (L * 3 + 1) * 44 + f + 1]
                    w2 = FM[:, COL_CW + (L * 3 + 2) * 44 + f:COL_CW + (L * 3 + 2) * 44 + f + 1]
                    bb = FM[:, COL_CBI + L * 44 + f:COL_CBI + L * 44 + f + 1]
                    rdu = [tU[which][c2] for c2 in (tc - 1, tc, tc + 1) if 0 <= c2 < ntc] + [tPad, tFM]
                    trc = tCc[which][tc]
                    p.op("act", lambda e, U_=U_, Cc=Cc, w0=w0, bb=bb: e.activation(Cc[:, co:co + n], U_[:, co - 1:co + n - 1], AF.Identity, bias=bb, scale=w0),
                         reads=rdu, writes=[trc])
                    p.op("dve", lambda e, U_=U_, Cc=Cc, w1=w1: e.scalar_tensor_tensor(Cc[:, co:co + n], U_[:, co:co + n], w1, Cc[:, co:co + n], ALU.mult, ALU.add),
                         reads=rdu + [trc], writes=[trc])
                    p.op("dve", lambda e, U_=U_, Cc=Cc, w2=w2: e.scalar_tensor_tensor(Cc[:, co:co + n], U_[:, co + 1:co + n + 1], w2, Cc[:, co:co + n], ALU.mult, ALU.add),
                         reads=rdu + [trc], writes=[trc])
                p.op("act", lambda e: e.activation(CG[:, co:co + n], CG[:, co:co + n], AF.Silu), reads=[tCc[0][tc]], writes=[tCc[0][tc]])
                p.op("pool", lambda e, AT=AT: e.tensor_tensor(AT[:, jl, co:co + n], CG[:, co:co + n], CV[:, co:co + n], ALU.mult),
                     reads=[tCc[0][tc], tCc[1][tc]], writes=[tAT[gi % 2][tc]])

            def down_item(gi, tc, m):
                WU, WD, trU, trD, bsel = bufs(gi)
                AT = actt(gi)
                t0, n = TCH[tc]
                co = colof(tc)
                s_ = b if tc < 4 else 2
                pb = 6 + (dnc[0] % 2)
                dnc[0] += 1
                for jl in range(2):
                    p.op("pe", lambda e, jl=jl: e.matmul(bank(pb)[:, 0:n], lhsT=WD[:, jl, m * 128:(m + 1) * 128], rhs=AT[:, jl, co:co + n],
                                                        start=(jl == 0), stop=(jl == 1)),
                         reads=trD + [tAT[gi % 2][tc]], writes=[tPS[pb]], sig=(jl == 1))
                p.op("dve", lambda e: e.scalar_tensor_tensor(Y[:, m, t0:t0 + n], bank(pb)[:, 0:n], Gcol(L, s_, 1, m), Y[:, m, t0:t0 + n], ALU.mult, ALU.add),
                     reads=[tPS[pb], tY[tc], tMOD], writes=[tY[tc]])

            NG = FC // 2
            steps = [(jl, tc) for jl in range(2) for tc in range(ntc)]
            ditems = [(tc, m) for tc in range(ntc) for m in range(KC)]
            per = (len(ditems) + len(steps) - 1) // len(steps)
            loadU(0)
            for gi in range(NG):
                if gi + 1 < NG:
                    loadU(gi + 1)
                loadD(gi)
                for si, (jl, tc) in enumerate(steps):
                    up_mm(gi, jl, tc)
                    if tc > 0:
                        conv(gi, jl, tc - 1)
                    if tc == ntc - 1:
                        conv(gi, jl, tc)
                    if gi > 0:
                        for (tcd, m) in ditems[si * per:(si + 1) * per]:
                            down_item(gi - 1, tcd, m)
            for (tcd, m) in ditems:
                down_item(NG - 1, tcd, m)

        for b in range(NB):
            load_x(b)
            ckpt(2)
            p.barrier()
            qkv_prefetch(0, 0, 0)
            for tc in range(5):
                norm_chunk(0, b if tc < 4 else 2, 0, tc)
            ckpt(3)
            for L in range(n_layers):
                kind = L % 3
                jj = L // 3
                last = (L == n_layers - 1)
                p.barrier()
                if "noattn" not in dbg:
                    qkv_phase(L, kind, jj, not last)
                p.barrier()
                ckpt(4)
                outproj_prefetch(kind, jj)
                for qc in range(4 if "noattn" not in dbg else 0):
                    if kind == 2:
                        st = []
                        for si in range(4 * qc - 1, 4 * qc + 5):
                            if 0 <= si < 16:
                                delta = si * 128 - qc * 512
                                st.append((si, 512 - delta))
                        st += [(16, None), (17, None)]
                    else:
                        st = [(si, None) for si in range(NTILE)]
                    attention(L, kind, jj, qc * 512, 512, qc, st)
                if not last and "noattn" not in dbg:
                    attention(L, kind, jj, T, C, 4, [(16, None), (17, None)])
                p.barrier()
                ckpt(5)
                outproj_phase(b, L, kind, jj, last)
                p.barrier()
                ckpt(6)
                if "noffn" not in dbg:
                    ffn_phase(b, L, last)
                p.barrier()
                ckpt(7)
                if not last:
                    for tc in range(5):
                        t0, n = TCH[tc]
                        p.dma("sp", XT[b][:, :, t0:t0 + n], Y[:, :, t0:t0 + n], f"d_xst{tc}", reads=[tY[tc]], writes=[tXT[tc]])
                    qkv_prefetch(L + 1, (L + 1) % 3, (L + 1) // 3)
                    for tc in range(5):
                        norm_chunk(L + 1, b if tc < 4 else 2, 0, tc)
                else:
                    store_out(b)
            p.barrier()
            ckpt(8)
        p.barrier()
        print(f"[kernel] instructions={p.n_ins} waits={p.n_wait} sems={len(p.sems)} counts={ {k: v for k, v in p.cnt.items() if k.startswith('e_')} }")
    return nc


_NC_CACHE = {}


def kernel(**inputs):
    n_cores = 8
    if "nc" not in _NC_CACHE:
        _NC_CACHE["nc"] = build_nc(DEPTH)
    nc = _NC_CACHE["nc"]
    hc = host_consts()
    shared = {k: np.ascontiguousarray(np.asarray(inputs[k], dtype=np.float32)) for k in WEIGHT_SHAPES}
    shared["c_ctx"] = np.ascontiguousarray(np.asarray(inputs["c_ctx"], dtype=np.float32))
    shared.update(hc)
    x = np.asarray(inputs["x"], dtype=np.float32)
    c = np.asarray(inputs["c"], dtype=np.float32)
    ctx = np.asarray(inputs["ctx"], dtype=np.float32)
    in_maps = []
    for i in range(n_cores):
        m = dict(shared)
        m["x"] = np.ascontiguousarray(x[i * NB:(i + 1) * NB])
        m["c"] = np.ascontiguousarray(c[i * NB:(i + 1) * NB])
        m["ctx"] = np.ascontiguousarray(ctx[i * NB:(i + 1) * NB])
        in_maps.append(m)
    res = run_bass_kernel_spmd(nc, in_maps, core_ids=list(range(n_cores)))
    return np.concatenate([np.asarray(r["out"], dtype=np.float32) for r in res.results], axis=0)
```

```python
import math
import numpy as np
from contextlib import ExitStack
import concourse.bass as bass
import concourse.mybir as mybir
from concourse.bass_utils import run_bass_kernel_spmd

F32 = mybir.dt.float32
BF16 = mybir.dt.bfloat16
ALU = mybir.AluOpType
AF = mybir.ActivationFunctionType
AX = mybir.AxisListType

NB = 2
T = 2048
C = 256
NT = T + C
D = 1024
KC = 8
DFF = 2816
FC = 22
DEPTH = 4
EPS = 1e-6
TCH = [(0, 512), (512, 512), (1024, 512), (1536, 512), (2048, 256)]
NTILE = NT // 128


class Trk:
    __slots__ = ("w", "r")

    def __init__(self):
        self.w = None
        self.r = []


class Prog:
    def __init__(self, nc, es):
        self.nc = nc
        self.es = es
        self.engs = {"pe": nc.tensor, "act": nc.scalar, "dve": nc.vector,
                     "pool": nc.gpsimd, "sp": nc.sync}
        self.sems = {}
        self.cnt = {}
        self.known = {k: {} for k in self.engs}
        for k in self.engs:
            if k != "sp":
                self._mksem("e_" + k)
        self.n_wait = 0
        self.n_ins = 0

    def _mksem(self, key):
        h = self.es.enter_context(self.nc.semaphore(key))
        self.sems[key] = h
        self.cnt[key] = 0
        return h

    def sbuf(self, name, shape, dt):
        return self.es.enter_context(self.nc.sbuf_tensor(name, list(shape), dt))

    def _emit_waits(self, eng, need):
        kn = self.known[eng]
        h = self.engs[eng]
        for k, v in need.items():
            if kn.get(k, 0) >= v:
                continue
            assert self.cnt[k] >= v, f"wait on future signal {k} {v} > {self.cnt[k]}"
            h.wait_ge(self.sems[k], v)
            kn[k] = v
            self.n_wait += 1

    def _waits(self, eng, reads, writes):
        need = {}
        me = "e_" + eng

        def add(dep, same_ok):
            if dep is None:
                return
            k, v = dep
            if same_ok and k == me:
                return
            if need.get(k, 0) < v:
                need[k] = v
        for t in reads:
            add(t.w, False)
        for t in writes:
            add(t.w, True)
            for d in t.r:
                add(d, True)
        self._emit_waits(eng, need)

    def _record(self, dep, reads, writes):
        for t in reads:
            if len(t.r) > 24:
                best = {}
                for k, v in t.r:
                    if best.get(k, 0) < v:
                        best[k] = v
                t.r = list(best.items())
            t.r.append(dep)
        for t in writes:
            t.w = dep
            t.r = []

    def op(self, eng, fn, reads=(), writes=(), sig=True):
        self._waits(eng, reads, writes)
        ins = fn(self.engs[eng])
        self.n_ins += 1
        k = "e_" + eng
        if sig:
            ins.then_inc(self.sems[k], 1)
            self.cnt[k] += 1
            dep = (k, self.cnt[k])
        else:
            dep = (k, self.cnt[k] + 1)
        self._record(dep, reads, writes)
        return dep

    def dma(self, eng, out, in_, semkey, reads=(), writes=(), **kw):
        if semkey not in self.sems:
            self._mksem(semkey)
        self._waits(eng, reads, writes)
        ins = self.engs[eng].dma_start(out=out, in_=in_, **kw)
        ins.then_inc(self.sems[semkey], 16)
        self.cnt[semkey] += 16
        dep = (semkey, self.cnt[semkey])
        self.n_ins += 1
        self._record(dep, reads, writes)
        return dep

    def barrier(self):
        for eng in self.engs:
            need = {k: v for k, v in self.cnt.items() if v > 0}
            self._emit_waits(eng, need)


def lam_init_of(L):
    return 0.8 - 0.6 * math.exp(-0.3 * L)


def host_consts():
    t = np.arange(T)
    row = (t // 64).astype(np.float32)
    col = (t % 64).astype(np.float32)
    n_freq = 16
    inv_freq = (np.float32(10000.0) ** (-np.arange(n_freq, dtype=np.float32) / np.float32(n_freq))).astype(np.float32)
    ang = np.concatenate([row[:, None] * inv_freq, col[:, None] * inv_freq], axis=-1).astype(np.float32)
    cos = np.cos(ang).astype(np.float32)
    sin = np.sin(ang).astype(np.float32)
    pidx = (np.arange(128) % 64) % 32
    rope = np.stack([cos[:, pidx].T, sin[:, pidx].T], 0).astype(np.float32)
    rmat = np.zeros((128, 128), np.float32)
    for m in range(128):
        if (m % 64) < 32:
            rmat[m + 32, m] = -1.0
        else:
            rmat[m - 32, m] = 1.0
    sl = np.arange(128)[:, None]
    u = np.arange(1152)[None, :]
    bandw = (np.abs(sl - (u - 512)) <= 128).astype(np.float32)
    return {"rope": np.ascontiguousarray(rope), "rmat": rmat, "bandw": np.ascontiguousarray(bandw)}


WEIGHT_SHAPES = {
    "adaln_w": (4, 1024, 6144), "adaln_b": (4, 6144), "norm1_g": (4, 1024), "norm2_g": (4, 1024),
    "ffn_w_up": (4, 1024, 5632), "ffn_conv_w": (4, 3, 5632), "ffn_conv_b": (4, 5632),
    "ffn_w_down": (4, 2816, 1024), "a_w_qkv": (2, 1024, 3072), "a_qk_g": (2, 2, 64),
    "a_lambda": (2, 4, 64), "a_head_g": (2, 128), "a_w_o": (2, 1024, 1024),
    "b_w_qkv": (1, 1024, 1536), "b_qk_g": (1, 2, 64), "b_w_o": (1, 1024, 1024),
    "c_w_qkv": (1, 1024, 1536), "c_qk_g": (1, 2, 64), "c_sink": (1, 16), "c_w_o": (1, 1024, 1024),
}


class _Stop(Exception):
    pass


def build_nc(n_layers=DEPTH, stop_after=None, dbg=""):
    holder = {}
    try:
        _build_inner(holder, n_layers, stop_after, dbg)
    except _Stop:
        pass
    return holder['nc']


def _build_inner(holder, n_layers, stop_after, dbg=""):
    nc = bass.Bass("TRN2", target_bir_lowering=False)
    holder['nc'] = nc

    def din(name, shape):
        return nc.dram_tensor(name, list(shape), F32, kind="ExternalInput").ap()

    x_d = din("x", [NB, T, D])
    c_d = din("c", [NB, D])
    ctx_d = din("ctx", [NB, C, D])
    cctx_d = din("c_ctx", [D])
    W = {k: din(k, s) for k, s in WEIGHT_SHAPES.items()}
    rope_d = din("rope", [2, 128, T])
    rmat_d = din("rmat", [128, 128])
    bandw_d = din("bandw", [128, 1152])
    out_d = nc.dram_tensor("out", [NB, T, D], F32, kind="ExternalOutput").ap()
    XT = nc.dram_tensor("xt_scratch", [NB, 128, KC, NT], F32, kind="Internal").ap()

    with ExitStack() as es:
        p = Prog(nc, es)
        PS = es.enter_context(nc.psum_tensor("ps", [128, 8, 512], F32))
        tPS = [Trk() for _ in range(8)]
        R1 = p.sbuf("r1", [128, KC, NT], BF16)
        R23 = p.sbuf("r23", [128, 2, KC, NT], BF16)
        QT = R23[:, 0]
        KT = R23[:, 1]
        Y = R23[:].rearrange("p a k t -> p (a k t)").bitcast(F32).rearrange("p (k t) -> p k t", k=KC)
        R4 = p.sbuf("r4", [128, 18944], BF16)
        R4f = R4[:].bitcast(F32)
        WA = p.sbuf("wa", [128, 12288], BF16)
        tWA = [Trk() for _ in range(6)]
        TM = p.sbuf("tm", [128, 10, 512], F32)
        tTM = [Trk() for _ in range(10)]
        TMflat = TM[:].rearrange("p a n -> p (a n)")
        FM = p.sbuf("fm", [128, 1024], F32)
        tFM = Trk()
        MOD = p.sbuf("mod", [128, 4, 3, 48], F32)
        AA = p.sbuf("aa", [128, 4, 3, 2, 8], F32)
        tMOD = Trk()
        IDENT = p.sbuf("ident_sb", [128, 128], F32)
        CB = p.sbuf("cb", [128, 5, 128], BF16)
        BANDW = p.sbuf("bandw_sb", [128, 1152], BF16)
        SCT = p.sbuf("sct", [128, KC, 3], BF16)
        FTMP = p.sbuf("ftmp", [128, 2, 512], F32)
        tFT = [Trk(), Trk()]
        ftc = [0]
        ONESF = p.sbuf("onesf", [128, 128], F32)
        attn_it = [0]
        SM = p.sbuf("sm", [128, 16], F32)
        LAMB = p.sbuf("lamb", [128, 2, 256], F32)
        LTMP = p.sbuf("ltmp", [128, 2, 2, 64], F32)
        LS = p.sbuf("ls", [128, 2, 2], F32)
        tC = Trk()
        tH = [Trk() for _ in range(5)]
        tQ = [[Trk() for _ in range(5)] for _ in range(KC)]
        tK = [[Trk() for _ in range(5)] for _ in range(KC)]
        tV = [Trk() for _ in range(NTILE)]
        tY = [Trk() for _ in range(5)]
        tXT = [Trk() for _ in range(5)]
        tR4 = [Trk() for _ in range(4)]
        tACT = [Trk(), Trk()]

        def bank(i):
            return PS[:, i, :]

        def tmf(i, n=512):
            return TM[:, i, 0:n]

        def tmb(i, half, n=512):
            return TM[:, i, :].bitcast(BF16)[:, half * 512: half * 512 + n]

        ones_mean = CB[:, 0, :]
        blockdiag = CB[:, 1, :]
        ones128m = CB[:, 2, :]
        ones1 = CB[:, 3, :]
        rmat = CB[:, 4, :]
        epsc = SM[:, 0:1]

        def ckpt(i):
            if stop_after is not None and stop_after == i:
                raise _Stop()

        p.op("pool", lambda e: e.memset(IDENT[:], 0.0), writes=[tC])
        p.op("pool", lambda e: e.affine_select(IDENT[:], IDENT[:], pattern=[[-1, 128]], compare_op=ALU.not_equal,
                                               fill=1.0, base=0, channel_multiplier=1), reads=[tC], writes=[tC])
        p.op("dve", lambda e: e.memset(CB[:, 0, :], 1.0 / 1024.0), writes=[tC])
        p.op("dve", lambda e: e.memset(CB[:, 1, :], 0.0), writes=[tC])
        p.op("dve", lambda e: e.memset(CB[0:64, 1, 0:64], 1.0 / 64.0), writes=[tC])
        p.op("dve", lambda e: e.memset(CB[64:128, 1, 64:128], 1.0 / 64.0), writes=[tC])
        p.op("dve", lambda e: e.memset(CB[:, 2, :], 1.0 / 128.0), writes=[tC])
        p.op("dve", lambda e: e.memset(CB[:, 3, :], 1.0), writes=[tC])
        p.op("dve", lambda e: e.memset(ONESF[:], 1.0), writes=[tC])
        p.op("dve", lambda e: e.memset(SM[:], 0.0), writes=[tC])
        p.op("dve", lambda e: e.memset(SM[:, 0:1], EPS), writes=[tC])
        p.dma("pool", CB[:, 4, :], rmat_d, "d_c1", writes=[tC])
        p.dma("pool", BANDW[:], bandw_d, "d_c2", writes=[tC])

        def stg(j):
            return TM[:, j // 4, (j % 4) * 128:(j % 4) * 128 + 128]
        tST = Trk()
        p.op("pool", lambda e: e.memset(TMflat[:, 0:4096], 0.0), writes=[tST])
        rowpos = [0]

        def add_rows(src2d, nrows):
            r0 = rowpos[0]
            done = 0
            while done < nrows:
                r = r0 + done
                j, off = r // 128, r % 128
                n = min(128 - off, nrows - done)
                p.dma("sp", stg(j)[off:off + n, :], src2d[done:done + n, :], "d_rows", writes=[tST])
                done += n
            rowpos[0] += nrows
            return r0
        COL_C = add_rows(c_d.rearrange("b (k q) -> (b k) q", q=128), 16)
        add_rows(cctx_d.rearrange("(k q) -> k q", q=128), 8)
        COL_AB = add_rows(W["adaln_b"].rearrange("l (m q) -> (l m) q", q=128), 192)
        COL_N1 = add_rows(W["norm1_g"].rearrange("l (m q) -> (l m) q", q=128), 32)
        COL_N2 = add_rows(W["norm2_g"].rearrange("l (m q) -> (l m) q", q=128), 32)
        COL_CW = add_rows(W["ffn_conv_w"].rearrange("l j (m q) -> (l j m) q", q=128), 528)
        COL_CBI = add_rows(W["ffn_conv_b"].rearrange("l (m q) -> (l m) q", q=128), 176)
        COL_HG = add_rows(W["a_head_g"], 2)
        COL_QKG = rowpos[0]
        for nm, nl in (("a_qk_g", 2), ("b_qk_g", 1), ("c_qk_g", 1)):
            src = W[nm].rearrange("l a d -> (l a) d")
            for r in range(nl * 2):
                rr = rowpos[0]
                j, off = rr // 128, rr % 128
                for hh in range(2):
                    p.dma("sp", stg(j)[off:off + 1, hh * 64:(hh + 1) * 64], src[r:r + 1, :], "d_rows", writes=[tST])
                rowpos[0] += 1
        assert rowpos[0] <= 1024
        QKG_COL = {("a", 0): COL_QKG, ("a", 1): COL_QKG + 2, ("b", 0): COL_QKG + 4, ("c", 0): COL_QKG + 6}
        for j in range(8):
            p.op("pe", lambda e, j=j: e.transpose(bank(j % 2)[:, 0:128], stg(j), IDENT[:]), reads=[tST, tC], writes=[tPS[j % 2]])
            p.op("dve", lambda e, j=j: e.tensor_copy(FM[:, j * 128:(j + 1) * 128], bank(j % 2)[:, 0:128]), reads=[tPS[j % 2]], writes=[tFM])
        p.op("act", lambda e: e.activation(SCT[:].rearrange("p k s -> p s k"), FM[:, 0:24].rearrange("p (s k) -> p s k", s=3), AF.Silu),
             reads=[tFM], writes=[tC])

        for j in range(2):
            L_ = 3 * j
            p.dma("sp", LAMB[:, j, :], W["a_lambda"][j].rearrange("a d -> (a d)").partition_broadcast(128), "d_lam", writes=[tC])
        for j in range(2):
            L_ = 3 * j
            lv = LAMB[:, j, :].rearrange("p (a b d) -> p a b d", a=2, b=2)
            p.op("dve", lambda e, lv=lv: e.tensor_tensor(LTMP[:, :, 0, :], lv[:, :, 0, :], lv[:, :, 1, :], ALU.mult), reads=[tC], writes=[tC])
            p.op("dve", lambda e: e.reduce_sum(LS[:, :, 0], LTMP[:, :, 0, :], axis=AX.X), reads=[tC], writes=[tC])
            p.op("act", lambda e: e.activation(LS[:, :, 1], LS[:, :, 0], AF.Exp), reads=[tC], writes=[tC])
            p.op("dve", lambda e, j=j: e.tensor_tensor(SM[:, 1 + j:2 + j], LS[:, 1, 1:2], LS[:, 0, 1:2], ALU.subtract), reads=[tC], writes=[tC])
            p.op("dve", lambda e, j=j, L_=L_: e.tensor_scalar(SM[:, 1 + j:2 + j], SM[:, 1 + j:2 + j], -lam_init_of(L_), None, ALU.add), reads=[tC], writes=[tC])
            p.op("dve", lambda e, j=j, L_=L_: e.tensor_scalar(SM[:, 3 + j:4 + j], FM[:, COL_HG + j:COL_HG + j + 1], 1.0 - lam_init_of(L_), None, ALU.mult),
                 reads=[tC, tFM], writes=[tC])
        for h in range(16):
            p.dma("sp", SM[(h % 2) * 64:(h % 2) * 64 + 64, 8 + h // 2:9 + h // 2], W["c_sink"][0][h:h + 1].partition_broadcast(64), "d_lam", writes=[tC])
        p.op("act", lambda e: e.activation(SM[:, 8:16], SM[:, 8:16], AF.Exp), reads=[tC], writes=[tC])

        ckpt(0)
        p.barrier()
        nload = [0]
        for L in range(n_layers):
            wv = W["adaln_w"][L].rearrange("(c q) n -> q c n", q=128)
            pb = L % 2
            for g in range(8):
                buf = nload[0] % 2
                nload[0] += 1
                wt = WA[:, buf * 6144:(buf + 1) * 6144].rearrange("p (c n) -> p c n", c=KC)
                trs = tWA[buf * 3:buf * 3 + 3]
                p.dma("pool", wt, wv[:, :, g * 768:(g + 1) * 768], f"d_w{buf}", writes=trs)
                for mm in range(6):
                    m = g * 6 + mm
                    for k in range(KC):
                        p.op("pe", lambda e, k=k, m=m, mm=mm, wt=wt, pb=pb: e.matmul(bank(pb)[:, m * 3:m * 3 + 3], lhsT=wt[:, k, mm * 128:(mm + 1) * 128],
                                                                             rhs=SCT[:, k, :], start=(k == 0), stop=(k == KC - 1)),
                             reads=trs + [tC], writes=[tPS[pb]], sig=(k == KC - 1))
            for s in range(3):
                src = bank(pb)[:, 0:144].rearrange("p (m s) -> p s m", s=3)[:, s, :]
                p.op("dve", lambda e, s=s, src=src, L=L: e.tensor_tensor(MOD[:, L, s, :], src, FM[:, COL_AB + L * 48:COL_AB + (L + 1) * 48], ALU.add),
                     reads=[tPS[pb], tFM], writes=[tMOD])
                for w_, (mo, gc) in enumerate(((8, COL_N1), (32, COL_N2))):
                    p.op("dve", lambda e, s=s, L=L, w_=w_, mo=mo, gc=gc: e.scalar_tensor_tensor(AA[:, L, s, w_, :], MOD[:, L, s, mo:mo + 8], 1.0,
                                                                                           FM[:, gc + L * 8:gc + L * 8 + 8], ALU.add, ALU.mult),
                         reads=[tMOD, tFM], writes=[tMOD])
        p.barrier()

        ckpt(1)

        def Acol(L, s, w_, k):
            return AA[:, L, s, w_, k:k + 1]

        def Bcol(L, s, w_, k):
            return MOD[:, L, s, (0 if w_ == 0 else 24) + k:(0 if w_ == 0 else 24) + k + 1]

        def Gcol(L, s, w_, k):
            return MOD[:, L, s, (16 if w_ == 0 else 40) + k:(16 if w_ == 0 else 40) + k + 1]

        def norm_chunk(L, s, w_, tc):
            t0, n = TCH[tc]
            pb = 6 + (tc % 2)
            for k in range(KC):
                sq = tmb(0, k % 2, n)
                p.op("act", lambda e, k=k, sq=sq: e.activation(sq, Y[:, k, t0:t0 + n], AF.Square), reads=[tY[tc]], writes=[tTM[0]])
                p.op("pe", lambda e, k=k, sq=sq: e.matmul(bank(pb)[:, 0:n], lhsT=ones_mean, rhs=sq, start=(k == 0), stop=(k == KC - 1)),
                     reads=[tTM[0], tC], writes=[tPS[pb]], sig=True)
            p.op("act", lambda e: e.activation(tmf(1, n), bank(pb)[:, 0:n], AF.Ln, bias=epsc, scale=1.0), reads=[tPS[pb], tC], writes=[tTM[1]])
            p.op("act", lambda e: e.activation(tmf(2, n), tmf(1, n), AF.Exp, scale=-0.5), reads=[tTM[1]], writes=[tTM[2]])
            for k in range(KC):
                sl = 3 + (k % 3)
                p.op("dve", lambda e, k=k, sl=sl: e.scalar_tensor_tensor(tmf(sl, n), Y[:, k, t0:t0 + n], Acol(L, s, w_, k), tmf(2, n), ALU.mult, ALU.mult),
                     reads=[tY[tc], tTM[2], tMOD], writes=[tTM[sl]])
                p.op("act", lambda e, k=k, sl=sl: e.activation(R1[:, k, t0:t0 + n], tmf(sl, n), AF.Identity, bias=Bcol(L, s, w_, k), scale=1.0),
                     reads=[tTM[sl], tMOD], writes=[tH[tc]])

        def load_x(b):
            for tc in range(5):
                t0, n = TCH[tc]
                nt = n // 128
                sb = tc % 2
                xs = R4f[:, sb * 4096:(sb + 1) * 4096].rearrange("p (i f) -> p i f", f=D)
                src = x_d[b][t0:t0 + n] if tc < 4 else ctx_d[b]
                p.dma("sp", xs[:, 0:nt, :], src.rearrange("(i q) f -> q i f", q=128), f"d_xs{sb}", writes=[tR4[sb]])
                for k in range(KC):
                    pb = k % 4
                    for i in range(nt):
                        p.op("pe", lambda e, k=k, i=i, pb=pb, xs=xs: e.transpose(bank(pb)[:, i * 128:(i + 1) * 128], xs[:, i, k * 128:(k + 1) * 128], IDENT[:]),
                             reads=[tR4[sb], tC], writes=[tPS[pb]], sig=(i == nt - 1))
                    if k % 2 == 0:
                        p.op("dve", lambda e, k=k, pb=pb: e.tensor_copy(Y[:, k, t0:t0 + n], bank(pb)[:, 0:n]), reads=[tPS[pb]], writes=[tY[tc]])
                    else:
                        p.op("act", lambda e, k=k, pb=pb: e.copy(Y[:, k, t0:t0 + n], bank(pb)[:, 0:n]), reads=[tPS[pb]], writes=[tY[tc]])
                p.dma("sp", XT[b][:, :, t0:t0 + n], Y[:, :, t0:t0 + n], f"d_xst{tc}", reads=[tY[tc]], writes=[tXT[tc]])

        def store_out(b):
            for tc in range(4):
                t0, n = TCH[tc]
                sb = tc % 2
                os_ = R4f[:, sb * 4096:(sb + 1) * 4096].rearrange("p (i f) -> p i f", f=D)
                cnt = 0
                for i in range(4):
                    for kg in range(2):
                        pb = cnt % 4
                        cnt += 1
                        for kk in range(4):
                            k = kg * 4 + kk
                            p.op("pe", lambda e, k=k, kk=kk, i=i, pb=pb: e.transpose(bank(pb)[:, kk * 128:(kk + 1) * 128], Y[:, k, t0 + i * 128:t0 + (i + 1) * 128], IDENT[:]),
                                 reads=[tY[tc], tC], writes=[tPS[pb]], sig=(kk == 3))
                        if cnt % 2 == 0:
                            p.op("dve", lambda e, i=i, kg=kg, pb=pb, os_=os_: e.tensor_copy(os_[:, i, kg * 512:(kg + 1) * 512], bank(pb)), reads=[tPS[pb]], writes=[tR4[sb]])
                        else:
                            p.op("act", lambda e, i=i, kg=kg, pb=pb, os_=os_: e.copy(os_[:, i, kg * 512:(kg + 1) * 512], bank(pb)), reads=[tPS[pb]], writes=[tR4[sb]])
                p.dma("sp", out_d[b][t0:t0 + n].rearrange("(i q) f -> q i f", q=128), os_, f"d_out{sb}", reads=[tR4[sb]], writes=[tXT[tc]])

        qkset = [0]

        def qk_post(pb, n, dest, gcol, rope_t0, tdest):
            st = qkset[0] % 2
            qkset[0] += 1
            sa, sb_, sc_ = 3 * st, 3 * st + 1, 3 * st + 2
            msb, rotb = 4 + st, 6 + st
            sq = tmb(sa, 0, n)
            p.op("act", lambda e: e.activation(sq, bank(pb)[:, 0:n], AF.Square), reads=[tPS[pb]], writes=[tTM[sa]])
            p.op("pe", lambda e: e.matmul(bank(msb)[:, 0:n], lhsT=blockdiag, rhs=sq, start=True, stop=True), reads=[tTM[sa], tC], writes=[tPS[msb]])
            p.op("act", lambda e: e.activation(tmf(sb_, n), bank(msb)[:, 0:n], AF.Ln, bias=epsc, scale=1.0), reads=[tPS[msb], tC], writes=[tTM[sb_]])
            p.op("act", lambda e: e.activation(tmf(sb_, n), tmf(sb_, n), AF.Exp, scale=-0.5), reads=[tTM[sb_]], writes=[tTM[sb_]])
            g = FM[:, gcol:gcol + 1]
            if rope_t0 is None:
                p.op("dve", lambda e: e.scalar_tensor_tensor(dest, bank(pb)[:, 0:n], g, tmf(sb_, n), ALU.mult, ALU.mult),
                     reads=[tPS[pb], tTM[sb_], tFM], writes=[tdest])
                return
            qn = tmb(sa, 1, n)
            p.op("dve", lambda e: e.scalar_tensor_tensor(qn, bank(pb)[:, 0:n], g, tmf(sb_, n), ALU.mult, ALU.mult),
                 reads=[tPS[pb], tTM[sb_], tFM], writes=[tTM[sa]])
            p.op("pe", lambda e: e.matmul(bank(rotb)[:, 0:n], lhsT=rmat, rhs=qn, start=True, stop=True), reads=[tTM[sa], tC], writes=[tPS[rotb]])
            cosv = TMflat[:, 6 * 512:8 * 512].bitcast(BF16)[:, rope_t0:rope_t0 + n]
            sinv = TMflat[:, 8 * 512:10 * 512].bitcast(BF16)[:, rope_t0:rope_t0 + n]
            p.op("dve", lambda e: e.tensor_tensor(tmf(sc_, n), qn, cosv, ALU.mult), reads=[tTM[sa], tTM[6]], writes=[tTM[sc_]])
            p.op("dve", lambda e: e.tensor_tensor(tmf(sb_, n), bank(rotb)[:, 0:n], sinv, ALU.mult), reads=[tPS[rotb], tTM[6], tTM[sb_]], writes=[tTM[sb_]])
            p.op("pool", lambda e: e.tensor_tensor(dest, tmf(sc_, n), tmf(sb_, n), ALU.add), reads=[tTM[sc_], tTM[sb_]], writes=[tdest])

        def qkv_setup(L, kind, jj):
            nm = "abc"[kind]
            wq = W[nm + "_w_qkv"][jj].rearrange("(c q) n -> q c n", q=128)
            if kind == 0:
                groups = [("q", 0, 0), ("q", 512, 4), ("k", 1024, 0), ("k", 1536, 4), ("v", 2048, 0), ("v", 2560, 1)]
            else:
                groups = [("q", 0, 0), ("q", 512, 4), ("kd", 1024, 0), ("vg", 1280, 0)]

            def slot(i):
                return WA[:, (i % 3) * 4096:(i % 3 + 1) * 4096].rearrange("p (c n) -> p c n", c=KC), tWA[(i % 3) * 2:(i % 3) * 2 + 2]

            def load(i):
                ty, co, _ = groups[i]
                wt, trs = slot(i)
                if ty == "kd":
                    wt4 = wt.rearrange("p c (g d) -> p c g d", g=4)
                    for g_ in range(4):
                        for hh in range(2):
                            p.dma("pool", wt4[:, :, g_, hh * 64:(hh + 1) * 64], wq[:, :, co + g_ * 64:co + (g_ + 1) * 64], f"d_wq{i % 3}", writes=trs)
                elif ty == "vg":
                    p.dma("pool", wt[:, :, 0:256], wq[:, :, co:co + 256], f"d_wq{i % 3}", writes=trs)
                else:
                    p.dma("pool", wt, wq[:, :, co:co + 512], f"d_wq{i % 3}", writes=trs)
            return groups, slot, load

        def qkv_prefetch(L, kind, jj):
            groups, slot, load = qkv_setup(L, kind, jj)
            p.dma("pool", TMflat[:, 6 * 512:8 * 512].bitcast(BF16), rope_d[0], "d_rope0", writes=[tTM[6]])
            p.dma("pool", TMflat[:, 8 * 512:10 * 512].bitcast(BF16), rope_d[1], "d_rope1", writes=[tTM[6]])
            load(0)
            load(1)

        def qkv_phase(L, kind, jj, need_ctx):
            nm = "abc"[kind]
            gq = QKG_COL[(nm, 0)] + 2 * jj * (1 if nm == "a" else 0)
            gk = gq + 1
            groups, slot, load = qkv_setup(L, kind, jj)
            ng = len(groups)
            pbc = [0]
            items = []
            for gi in range(ng):
                ty, co, cb = groups[gi]
                if ty not in ("q", "k", "kd"):
                    continue
                for cc in range(4):
                    for tc in range(5):
                        if ty == "q" and tc == 4 and not need_ctx:
                            continue
                        items.append((gi, cc, tc, (cc == 0 and tc == 0)))
            state = {}

            def emit_main(i):
                gi, cc, tc, first = items[i]
                ty, co, cb = groups[gi]
                if first and gi + 2 < ng:
                    load(gi + 2)
                wt, trs = slot(gi)
                t0, n = TCH[tc]
                pb = pbc[0] % 4
                pbc[0] += 1
                state[i] = pb
                for k in range(KC):
                    p.op("pe", lambda e, k=k, cc=cc, pb=pb, wt=wt, t0=t0, n=n: e.matmul(bank(pb)[:, 0:n], lhsT=wt[:, k, cc * 128:(cc + 1) * 128],
                                                                                     rhs=R1[:, k, t0:t0 + n], start=(k == 0), stop=(k == KC - 1)),
                         reads=trs + [tH[tc]], writes=[tPS[pb]], sig=(k == KC - 1))

            def emit_post(i):
                gi, cc, tc, first = items[i]
                ty, co, cb = groups[gi]
                t0, n = TCH[tc]
                cdst = cb + cc
                pb = state[i]
                if ty == "q":
                    qk_post(pb, n, QT[:, cdst, t0:t0 + n], gq, t0 if tc < 4 else None, tQ[cdst][tc])
                else:
                    qk_post(pb, n, KT[:, cdst, t0:t0 + n], gk, t0 if tc < 4 else None, tK[cdst][tc])
            nxt = 0
            for i in range(len(items)):
                while nxt <= min(i + 2, len(items) - 1):
                    emit_main(nxt)
                    nxt += 1
                emit_post(i)
            for gi in range(ng):
                ty, co, cb = groups[gi]
                if ty in ("q", "k", "kd"):
                    continue
                if gi + 2 < ng:
                    load(gi + 2)
                wt, trs = slot(gi)
                ncol = 512 if ty == "v" else 256
                vw = 1024 if kind == 0 else 256
                Vv = R4[:, 0:NTILE * vw].rearrange("p (s e) -> p s e", e=vw)
                for ti in range(NTILE):
                    tc = min(ti // 4, 4)
                    pb = pbc[0] % 4
                    pbc[0] += 1
                    for k in range(KC):
                        p.op("pe", lambda e, k=k, pb=pb, ti=ti, wt=wt, ncol=ncol: e.matmul(bank(pb)[:, 0:ncol], lhsT=R1[:, k, ti * 128:(ti + 1) * 128],
                                                                                       rhs=wt[:, k, 0:ncol], start=(k == 0), stop=(k == KC - 1)),
                             reads=trs + [tH[tc]], writes=[tPS[pb]], sig=(k == KC - 1))
                    dst = Vv[:, ti, cb * 512:cb * 512 + ncol]
                    if ti % 2 == 0:
                        p.op("dve", lambda e, pb=pb, dst=dst, ncol=ncol: e.tensor_copy(dst, bank(pb)[:, 0:ncol]), reads=[tPS[pb]], writes=[tV[ti]])
                    else:
                        p.op("act", lambda e, pb=pb, dst=dst, ncol=ncol: e.copy(dst, bank(pb)[:, 0:ncol]), reads=[tPS[pb]], writes=[tV[ti]])

        def attention(L, kind, jj, q0, nq, tcq, stiles):
            vw = 1024 if kind == 0 else 256
            Vv = R4[:, 0:NTILE * vw].rearrange("p (s e) -> p s e", e=vw)
            ns = len(stiles)
            for c in range(KC):
                kc = c if kind == 0 else c // 2
                nring = 3 if kind == 0 else 2
                if kind == 0:
                    accb = [6, 7]
                    sumb = 2 * (ns % 3)
                else:
                    ab0 = 4 + 2 * (c % 2)
                    accb = [ab0, ab0 + 1]

                aslot = 6 + 2 * (attn_it[0] % 2)
                attn_it[0] += 1
                acc32 = TMflat[:, aslot * 512:(aslot + 2) * 512].rearrange("p (h n) -> p h n", h=2)

                def S(i):
                    ti, _ = stiles[i]
                    sb = 2 * (i % nring)
                    tck = min(ti // 4, 4)
                    for hh in range(2):
                        p.op("pe", lambda e, hh=hh, ti=ti, sb=sb: e.matmul(bank(sb + hh)[:, 0:nq], lhsT=KT[hh * 64:(hh + 1) * 64, kc, ti * 128:(ti + 1) * 128],
                                                                      rhs=QT[hh * 64:(hh + 1) * 64, c, q0:q0 + nq], start=True, stop=True),
                             reads=[tK[kc][tck], tQ[c][tcq]], writes=[tPS[sb], tPS[sb + 1]], sig=(hh == 1))

                def E(i):
                    ti, mo = stiles[i]
                    sb = 2 * (i % nring)
                    sl = i % 3
                    pt = TM[:, sl, :].bitcast(BF16).rearrange("p (h n) -> p h n", h=2)
                    p.op("act", lambda e, sb=sb, pt=pt: e.activation(pt[:, :, 0:nq], PS[:, sb:sb + 2, 0:nq], AF.Exp, scale=0.125),
                         reads=[tPS[sb], tPS[sb + 1]], writes=[tTM[sl]])
                    if mo is not None:
                        mk = BANDW[:, mo:mo + nq].unsqueeze(1).broadcast_to([128, 2, nq])
                        p.op("dve", lambda e, pt=pt, mk=mk: e.tensor_tensor(pt[:, :, 0:nq], pt[:, :, 0:nq], mk, ALU.mult),
                             reads=[tTM[sl], tC], writes=[tTM[sl]])

                def PV(i):
                    ti, _ = stiles[i]
                    sl = i % 3
                    pt = TM[:, sl, :].bitcast(BF16).rearrange("p (h n) -> p h n", h=2)
                    st, sp_ = (i == 0), (i == ns - 1)
                    wr = [tPS[b_] for b_ in accb]
                    if kind == 0:
                        vh = Vv[:, ti, c * 128:(c + 1) * 128]
                        if i == 0:
                            p.op("dve", lambda e, pt=pt: e.tensor_copy(acc32[:, :, 0:nq], pt[:, :, 0:nq]), reads=[tTM[sl]], writes=[tTM[aslot]])
                        else:
                            p.op("dve", lambda e, pt=pt: e.tensor_tensor(acc32[:, :, 0:nq], acc32[:, :, 0:nq], pt[:, :, 0:nq], ALU.add), reads=[tTM[sl], tTM[aslot]], writes=[tTM[aslot]])
                        for hh in range(2):
                            p.op("pe", lambda e, hh=hh, vh=vh, pt=pt: e.matmul(bank(accb[hh])[:, 0:nq], lhsT=vh, rhs=pt[:, hh, 0:nq], start=st, stop=sp_),
                                 reads=[tV[ti], tTM[sl]], writes=wr, sig=(hh == 1))
                        if sp_:
                            for hh in range(2):
                                p.op("pe", lambda e, hh=hh: e.matmul(bank(sumb + hh)[:, 0:nq], lhsT=ONESF[:], rhs=acc32[:, hh, 0:nq], start=True, stop=True),
                                     reads=[tC, tTM[aslot]], writes=[tPS[sumb], tPS[sumb + 1]], sig=(hh == 1))
                    else:
                        vg = Vv[:, ti, kc * 64:(kc + 1) * 64]
                        for hh in range(2):
                            p.op("pe", lambda e, hh=hh, vg=vg, pt=pt: e.matmul(bank(accb[0])[hh * 64:(hh + 1) * 64, 0:nq], lhsT=vg, rhs=pt[:, hh, 0:nq], start=st, stop=sp_,
                                                                         tile_position=(0, hh * 64)),
                                 reads=[tV[ti], tTM[sl]], writes=wr, sig=False)
                        for hh in range(2):
                            p.op("pe", lambda e, hh=hh, pt=pt: e.matmul(bank(accb[1])[hh * 64:(hh + 1) * 64, 0:nq], lhsT=ones1[:, 0:64], rhs=pt[:, hh, 0:nq], start=st, stop=sp_,
                                                                  tile_position=(0, hh * 64)),
                                 reads=[tC, tTM[sl]], writes=wr, sig=(hh == 1))

                for i0 in range(min(nring - 1, ns)):
                    S(i0)
                for i in range(ns):
                    if i + nring - 1 < ns:
                        S(i + nring - 1)
                    E(i)
                    PV(i)
                rd = [tPS[b_] for b_ in accb]
                dst = R1[:, c, q0:q0 + nq]
                if kind == 0:
                    rs_ = [tPS[sumb], tPS[sumb + 1]]
                    p.op("act", lambda e: e.activation(tmf(3, nq), bank(sumb)[:, 0:nq], AF.Ln), reads=rs_, writes=[tTM[3]])
                    p.op("act", lambda e: e.activation(tmf(4, nq), bank(sumb + 1)[:, 0:nq], AF.Ln), reads=rs_, writes=[tTM[4]])
                    p.op("act", lambda e: e.activation(tmf(3, nq), tmf(3, nq), AF.Exp, scale=-1.0), reads=[tTM[3]], writes=[tTM[3]])
                    p.op("act", lambda e: e.activation(tmf(4, nq), tmf(4, nq), AF.Exp, scale=-1.0), reads=[tTM[4]], writes=[tTM[4]])
                    p.op("dve", lambda e: e.tensor_tensor(tmf(3, nq), bank(accb[0])[:, 0:nq], tmf(3, nq), ALU.mult), reads=rd + [tTM[3]], writes=[tTM[3]])
                    p.op("dve", lambda e: e.tensor_tensor(tmf(4, nq), bank(accb[1])[:, 0:nq], tmf(4, nq), ALU.mult), reads=rd + [tTM[4]], writes=[tTM[4]])
                    p.op("dve", lambda e: e.scalar_tensor_tensor(tmf(4, nq), tmf(4, nq), SM[:, 1 + jj:2 + jj], tmf(3, nq), ALU.mult, ALU.add),
                         reads=[tTM[3], tTM[4], tC], writes=[tTM[4]])
                    sq = tmb(5, 0, nq)
                    p.op("act", lambda e: e.activation(sq, tmf(4, nq), AF.Square), reads=[tTM[4]], writes=[tTM[5]])
                    p.op("pe", lambda e: e.matmul(bank(accb[0])[:, 0:nq], lhsT=ones128m, rhs=sq, start=True, stop=True), reads=[tTM[5], tC], writes=rd)
                    p.op("act", lambda e: e.activation(tmf(3, nq), bank(accb[0])[:, 0:nq], AF.Ln, bias=epsc, scale=1.0), reads=rd + [tC, tTM[3]], writes=[tTM[3]])
                    p.op("act", lambda e: e.activation(tmf(3, nq), tmf(3, nq), AF.Exp, scale=-0.5), reads=[tTM[3]], writes=[tTM[3]])
                    p.op("dve", lambda e: e.scalar_tensor_tensor(dst, tmf(4, nq), SM[:, 3 + jj:4 + jj], tmf(3, nq), ALU.mult, ALU.mult),
                         reads=[tTM[4], tTM[3], tC], writes=[tH[tcq]])
                else:
                    if kind == 2:
                        p.op("act", lambda e: e.activation(tmf(3, nq), bank(accb[1])[:, 0:nq], AF.Ln, bias=SM[:, 8 + c:9 + c], scale=1.0), reads=rd + [tC], writes=[tTM[3]])
                    else:
                        p.op("act", lambda e: e.activation(tmf(3, nq), bank(accb[1])[:, 0:nq], AF.Ln), reads=rd, writes=[tTM[3]])
                    p.op("act", lambda e: e.activation(tmf(3, nq), tmf(3, nq), AF.Exp, scale=-1.0), reads=[tTM[3]], writes=[tTM[3]])
                    p.op("dve", lambda e: e.tensor_tensor(dst, bank(accb[0])[:, 0:nq], tmf(3, nq), ALU.mult), reads=rd + [tTM[3]], writes=[tH[tcq]])

        def outproj_prefetch(kind, jj):
            nm = "abc"[kind]
            wo = W[nm + "_w_o"][jj].rearrange("(c q) n -> q c n", q=128)
            WO = WA[:, 4096:12288].rearrange("p (c n) -> p c n", c=KC)
            p.dma("pool", WO, wo, "d_wo", writes=tWA[2:6])

        def outproj_phase(b, L, kind, jj, last):
            nm = "abc"[kind]
            wo = W[nm + "_w_o"][jj].rearrange("(c q) n -> q c n", q=128)
            WO = WA[:, 4096:12288].rearrange("p (c n) -> p c n", c=KC)
            trs = tWA[2:6]
            pbc = 0
            for tc in range(4 if last else 5):
                t0, n = TCH[tc]
                s = b if tc < 4 else 2
                p.dma("sp", Y[:, :, t0:t0 + n], XT[b][:, :, t0:t0 + n], f"d_xld{tc}", reads=[tXT[tc]], writes=[tY[tc]])
                for m in range(KC if "noattn" not in dbg else 0):
                    pb = pbc % 6
                    pbc += 1
                    for k in range(KC):
                        p.op("pe", lambda e, k=k, m=m, pb=pb: e.matmul(bank(pb)[:, 0:n], lhsT=WO[:, k, m * 128:(m + 1) * 128], rhs=R1[:, k, t0:t0 + n],
                                                                   start=(k == 0), stop=(k == KC - 1)),
                             reads=trs + [tH[tc]], writes=[tPS[pb]], sig=(k == KC - 1))
                    p.op("dve", lambda e, m=m, pb=pb: e.scalar_tensor_tensor(Y[:, m, t0:t0 + n], bank(pb)[:, 0:n], Gcol(L, s, 0, m), Y[:, m, t0:t0 + n], ALU.mult, ALU.add),
                         reads=[tPS[pb], tY[tc], tMOD], writes=[tY[tc]])
                norm_chunk(L, s, 1, tc)

        def ffn_phase(b, L, last):
            ntc = 4 if last else 5
            cend = 2050 if last else 2308
            UG = R4f[:, 0:2310]
            UV = R4f[:, 2310:4620]
            CG = R4f[:, 4620:6930]
            CV = R4f[:, 6930:9240]
            for U_, tr in ((UG, tR4[0]), (UV, tR4[1])):
                for a0 in (0, 2050, 2308):
                    p.op("pool", lambda e, U_=U_, a0=a0: e.memset(U_[:, a0:a0 + 2], 0.0), writes=[tr])
            wu = W["ffn_w_up"][L].rearrange("(c q) n -> q c n", q=128)
            wd = W["ffn_w_down"][L]
            tYm = [[Trk() for _ in range(KC)] for _ in range(5)]

            def colof(tc):
                return 2 + 512 * tc if tc < 4 else 2052

            def bufs(gi):
                bsel = gi % 2
                base = bsel * 6144
                WU = WA[:, base:base + 4096].rearrange("p (c n) -> p c n", c=KC)
                WD = WA[:, base + 4096:base + 6144].rearrange("p (j n) -> p j n", j=2)
                return WU, WD, tWA[bsel * 3:bsel * 3 + 2], tWA[bsel * 3 + 2:bsel * 3 + 3], bsel

            def loadU(gi):
                WU, WD, trU, trD, bsel = bufs(gi)
                j0 = gi * 2
                p.dma("pool", WU[:, :, 0:256], wu[:, :, j0 * 128:j0 * 128 + 256], f"d_fu{bsel}", writes=trU)
                p.dma("pool", WU[:, :, 256:512], wu[:, :, DFF + j0 * 128:DFF + j0 * 128 + 256], f"d_fu{bsel}", writes=trU)

            def loadD(gi):
                WU, WD, trU, trD, bsel = bufs(gi)
                j0 = gi * 2
                p.dma("pool", WD, wd[j0 * 128:j0 * 128 + 256, :].rearrange("(j q) n -> q j n", q=128), f"d_fd{bsel}", writes=trD)

            def actt(gi):
                r = gi % 2
                return TMflat[:, r * 2560:(r + 1) * 2560].bitcast(BF16)[:, 0:4620].rearrange("p (j n) -> p j n", j=2)

            pbc = [0]

            def up(gi):
                WU, WD, trs, trD, bsel = bufs(gi)
                AT = actt(gi)
                for jl in range(2):
                    j = gi * 2 + jl
                    for tc in range(ntc):
                        t0, n = TCH[tc]
                        co = colof(tc)
                        for which, U_, tr in ((0, UG, tR4[0]), (1, UV, tR4[1])):
                            pb = pbc[0] % 6
                            pbc[0] += 1
                            wc = which * 256 + jl * 128
                            for k in range(KC):
                                p.op("pe", lambda e, k=k, pb=pb, wc=wc, WU=WU, t0=t0, n=n: e.matmul(bank(pb)[:, 0:n], lhsT=WU[:, k, wc:wc + 128], rhs=R1[:, k, t0:t0 + n],
                                                                                                 start=(k == 0), stop=(k == KC - 1)),
                                     reads=trs + [tH[tc]], writes=[tPS[pb]], sig=(k == KC - 1))
                            p.op("act", lambda e, pb=pb, U_=U_, co=co, n=n: e.copy(U_[:, co:co + n], bank(pb)[:, 0:n]), reads=[tPS[pb]], writes=[tr])
                    for which, U_, Cc, eng, tru, trc in ((0, UG, CG, "dve", tR4[0], tR4[2]), (1, UV, CV, "dve", tR4[1], tR4[3])):
                        f = which * FC + j
                        w0 = FM[:, COL_CW + (L * 3 + 0) * 44 + f:COL_CW + (L * 3 + 0) * 44 + f + 1]
                        w1 = FM[:, COL_CW + (L * 3 + 1) * 44 + f:COL_CW + (L * 3 + 1) * 44 + f + 1]
                        w2 = FM[:, COL_CW + (L * 3 + 2) * 44 + f:COL_CW + (L * 3 + 2) * 44 + f + 1]
                        bb = FM[:, COL_CBI + L * 44 + f:COL_CBI + L * 44 + f + 1]
                        p.op("act", lambda e, U_=U_, Cc=Cc, w0=w0, bb=bb: e.activation(Cc[:, 2:cend], U_[:, 1:cend - 1], AF.Identity, bias=bb, scale=w0),
                             reads=[tru, tFM], writes=[trc])
                        p.op(eng, lambda e, U_=U_, Cc=Cc, w1=w1: e.scalar_tensor_tensor(Cc[:, 2:cend], U_[:, 2:cend], w1, Cc[:, 2:cend], ALU.mult, ALU.add),
                             reads=[tru, trc, tFM], writes=[trc])
                        p.op(eng, lambda e, U_=U_, Cc=Cc, w2=w2: e.scalar_tensor_tensor(Cc[:, 2:cend], U_[:, 3:cend + 1], w2, Cc[:, 2:cend], ALU.mult, ALU.add),
                             reads=[tru, trc, tFM], writes=[trc])
                    p.op("act", lambda e: e.activation(CG[:, 2:cend], CG[:, 2:cend], AF.Silu), reads=[tR4[2]], writes=[tR4[2]])
                    p.op("pool", lambda e, AT=AT, jl=jl: e.tensor_tensor(AT[:, jl, 2:cend], CG[:, 2:cend], CV[:, 2:cend], ALU.mult),
                         reads=[tR4[2], tR4[3]], writes=[tACT[gi % 2]])

            def down(gi):
                WU, WD, trU, trs, bsel = bufs(gi)
                AT = actt(gi)
                for tc in range(ntc):
                    t0, n = TCH[tc]
                    co = colof(tc)
                    s = b if tc < 4 else 2
                    for m in range(KC):
                        pb = 6 + (m % 2)
                        for jl in range(2):
                            p.op("pe", lambda e, jl=jl, m=m, pb=pb, WD=WD, AT=AT, co=co, n=n: e.matmul(bank(pb)[:, 0:n], lhsT=WD[:, jl, m * 128:(m + 1) * 128], rhs=AT[:, jl, co:co + n],
                                                                                                  start=(jl == 0), stop=(jl == 1)),
                                 reads=trs + [tACT[gi % 2]], writes=[tPS[pb]], sig=(jl == 1))
                        if m in (2, 5):
                            fi = ftc[0] % 2
                            ftc[0] += 1
                            p.op("act", lambda e, m=m, pb=pb, n=n, s=s, fi=fi: e.activation(FTMP[:, fi, 0:n], bank(pb)[:, 0:n], AF.Identity, scale=Gcol(L, s, 1, m)),
                                 reads=[tPS[pb], tMOD], writes=[tFT[fi]])
                            p.op("pool", lambda e, m=m, t0=t0, n=n, fi=fi: e.tensor_tensor(Y[:, m, t0:t0 + n], Y[:, m, t0:t0 + n], FTMP[:, fi, 0:n], ALU.add),
                                 reads=[tFT[fi], tYm[tc][m]], writes=[tYm[tc][m]])
                        else:
                            p.op("dve", lambda e, m=m, pb=pb, t0=t0, n=n, s=s: e.scalar_tensor_tensor(Y[:, m, t0:t0 + n], bank(pb)[:, 0:n], Gcol(L, s, 1, m), Y[:, m, t0:t0 + n], ALU.mult, ALU.add),
                                 reads=[tPS[pb], tYm[tc][m], tMOD], writes=[tYm[tc][m]])

            NG = FC // 2
            loadU(0)
            for gi in range(NG):
                if gi + 1 < NG:
                    loadU(gi + 1)
                loadD(gi)
                up(gi)
                if gi > 0:
                    down(gi - 1)
            down(NG - 1)

        for b in range(NB):
            load_x(b)
            ckpt(2)
            p.barrier()
            qkv_prefetch(0, 0, 0)
            for tc in range(5):
                norm_chunk(0, b if tc < 4 else 2, 0, tc)
            ckpt(3)
            for L in range(n_layers):
                kind = L % 3
                jj = L // 3
                last = (L == n_layers - 1)
                p.barrier()
                if "noattn" not in dbg:
                    qkv_phase(L, kind, jj, not last)
                p.barrier()
                ckpt(4)
                outproj_prefetch(kind, jj)
                for qc in range(4 if "noattn" not in dbg else 0):
                    if kind == 2:
                        st = []
                        for si in range(4 * qc - 1, 4 * qc + 5):
                            if 0 <= si < 16:
                                delta = si * 128 - qc * 512
                                st.append((si, 512 - delta))
                        st += [(16, None), (17, None)]
                    else:
                        st = [(si, None) for si in range(NTILE)]
                    attention(L, kind, jj, qc * 512, 512, qc, st)
                if not last and "noattn" not in dbg:
                    attention(L, kind, jj, T, C, 4, [(16, None), (17, None)])
                p.barrier()
                ckpt(5)
                outproj_phase(b, L, kind, jj, last)
                p.barrier()
                ckpt(6)
                if "noffn" not in dbg:
                    ffn_phase(b, L, last)
                p.barrier()
                ckpt(7)
                if not last:
                    for tc in range(5):
                        t0, n = TCH[tc]
                        p.dma("sp", XT[b][:, :, t0:t0 + n], Y[:, :, t0:t0 + n], f"d_xst{tc}", reads=[tY[tc]], writes=[tXT[tc]])
                    qkv_prefetch(L + 1, (L + 1) % 3, (L + 1) // 3)
                    for tc in range(5):
                        norm_chunk(L + 1, b if tc < 4 else 2, 0, tc)
                else:
                    store_out(b)
            p.barrier()
            ckpt(8)
        p.barrier()
        print(f"[kernel] instructions={p.n_ins} waits={p.n_wait} sems={len(p.sems)} counts={ {k: v for k, v in p.cnt.items() if k.startswith('e_')} }")
    return nc


_NC_CACHE = {}


def kernel(**inputs):
    n_cores = 8
    if "nc" not in _NC_CACHE:
        _NC_CACHE["nc"] = build_nc(DEPTH)
    nc = _NC_CACHE["nc"]
    hc = host_consts()
    shared = {k: np.ascontiguousarray(np.asarray(inputs[k], dtype=np.float32)) for k in WEIGHT_SHAPES}
    shared["c_ctx"] = np.ascontiguousarray(np.asarray(inputs["c_ctx"], dtype=np.float32))
    shared.update(hc)
    x = np.asarray(inputs["x"], dtype=np.float32)
    c = np.asarray(inputs["c"], dtype=np.float32)
    ctx = np.asarray(inputs["ctx"], dtype=np.float32)
    in_maps = []
    for i in range(n_cores):
        m = dict(shared)
        m["x"] = np.ascontiguousarray(x[i * NB:(i + 1) * NB])
        m["c"] = np.ascontiguousarray(c[i * NB:(i + 1) * NB])
        m["ctx"] = np.ascontiguousarray(ctx[i * NB:(i + 1) * NB])
        in_maps.append(m)
    res = run_bass_kernel_spmd(nc, in_maps, core_ids=list(range(n_cores)))
    return np.concatenate([np.asarray(r["out"], dtype=np.float32) for r in res.results], axis=0)
```
